# Optimizing a Trainium2 kernel written in Bass

```python
import math
import jax, jax.numpy as jnp
from jax import lax
import numpy as np

D_MODEL = 2048
BATCH = 2
SEQ = 4096
DEPTH = 4

HEAD_DIM = 128
N_HEADS = D_MODEL // HEAD_DIM
N_SB_HEADS = N_HEADS // 2
N_DIL_HEADS = N_HEADS - N_SB_HEADS
N_FOX_HEADS = N_HEADS
DILATED_CONFIGS = ((128, 1), (512, 4), (2048, 16))
ROPE_THETA = 500000.0
ROPE_DIM = HEAD_DIM // 4
D_FF = 256 * ((8 * D_MODEL // 3 + 255) // 256)
CONV_WIDTH = 3
PLE_DIM = 256
Q_BLOCK = 128
RMS_EPS = 1e-6

kernel_name = "hybrid_stickbreak_dilated_fox_convffn"


def rmsnorm(x, g):
    x32 = x.astype(jnp.float32)
    y = x32 * lax.rsqrt(jnp.mean(x32 * x32, axis=-1, keepdims=True) + RMS_EPS)
    return (y * g.astype(jnp.float32)).astype(x.dtype)


def partial_rope(x, positions):
    half = ROPE_DIM // 2
    inv_freq = ROPE_THETA ** (-jnp.arange(0, ROPE_DIM, 2, dtype=jnp.float32) / ROPE_DIM)
    ang = positions.astype(jnp.float32)[..., None] * inv_freq
    cos = jnp.cos(ang)[:, :, None, :].astype(x.dtype)
    sin = jnp.sin(ang)[:, :, None, :].astype(x.dtype)
    x1 = x[..., :half]
    x2 = x[..., half:ROPE_DIM]
    return jnp.concatenate([x1 * cos - x2 * sin, x2 * cos + x1 * sin, x[..., ROPE_DIM:]], axis=-1)


def stick_breaking_attention(q, k, v):
    B, S, H, dh = q.shape
    nb = S // Q_BLOCK
    scale = dh ** -0.5
    qb = q.reshape(B, nb, Q_BLOCK, H, dh).transpose(1, 0, 2, 3, 4)
    kpos = jnp.arange(S)

    def one_block(args):
        qblk, b_idx = args
        z = jnp.einsum('bqhd,bkhd->bhqk', qblk, k).astype(jnp.float32) * scale
        qpos = b_idx * Q_BLOCK + jnp.arange(Q_BLOCK)
        mask = kpos[None, :] < qpos[:, None]
        log_1mb = jnp.where(mask, jax.nn.log_sigmoid(-z), 0.0)
        suffix = lax.cumsum(log_1mb, axis=3, reverse=True) - log_1mb
        w = jnp.where(mask, jnp.exp(jax.nn.log_sigmoid(z) + suffix), 0.0)
        return jnp.einsum('bhqk,bkhd->bqhd', w.astype(v.dtype), v)

    out = lax.map(one_block, (qb, jnp.arange(nb)))
    return out.transpose(1, 0, 2, 3, 4).reshape(B, S, H, dh)


def banded_window_attention(q, k, v, n_back):
    B, G, L, H, dh = q.shape
    Lp = -(-L // Q_BLOCK) * Q_BLOCK
    padw = ((0, 0), (0, 0), (0, Lp - L), (0, 0), (0, 0))
    q, k, v = jnp.pad(q, padw), jnp.pad(k, padw), jnp.pad(v, padw)
    nb = Lp // Q_BLOCK
    qb = q.reshape(B, G, nb, Q_BLOCK, H, dh)
    kb = k.reshape(B, G, nb, Q_BLOCK, H, dh)
    vb = v.reshape(B, G, nb, Q_BLOCK, H, dh)
    blk_pad = ((0, 0), (0, 0), (1, 0), (0, 0), (0, 0), (0, 0))
    kk = jnp.concatenate([jnp.pad(kb, blk_pad)[:, :, :nb], kb], axis=3)
    vv = jnp.concatenate([jnp.pad(vb, blk_pad)[:, :, :nb], vb], axis=3)
    s = jnp.einsum('bgnqhd,bgnkhd->bgnhqk', qb, kk).astype(jnp.float32) * (dh ** -0.5)
    qi = jnp.arange(Q_BLOCK)[:, None] + Q_BLOCK
    kj = jnp.arange(2 * Q_BLOCK)[None, :]
    dist = qi - kj
    band = (dist >= 0) & (dist <= n_back)
    blk = jnp.arange(nb)
    valid = band[None] & ((blk[:, None, None] > 0) | (kj[None] >= Q_BLOCK))
    s = jnp.where(valid[:, None], s, -jnp.inf)
    lse = jax.nn.logsumexp(s, axis=-1)
    pr = jnp.exp(s - lse[..., None])
    out = jnp.einsum('bgnhqk,bgnkhd->bgnqhd', pr.astype(v.dtype), vv)
    out = out.reshape(B, G, Lp, H, dh)[:, :, :L]
    lse = lse.transpose(0, 1, 2, 4, 3).reshape(B, G, Lp, H)[:, :, :L]
    return out, lse


def to_stride_groups(x, dil):
    B, S, H, dh = x.shape
    return x.reshape(B, S // dil, dil, H, dh).transpose(0, 2, 1, 3, 4)


def dilated_attention(q, k, v):
    B, S, H, dh = q.shape
    outs, lses = [], []
    for window, dil in DILATED_CONFIGS:
        o, l = banded_window_attention(to_stride_groups(q, dil), to_stride_groups(k, dil),
                                       to_stride_groups(v, dil), window // dil)
        outs.append(o.transpose(0, 2, 1, 3, 4).reshape(B, S, H, dh))
        lses.append(l.transpose(0, 2, 1, 3).reshape(B, S, H))
    w = jax.nn.softmax(jnp.stack(lses, axis=0), axis=0)
    out = jnp.einsum('cbsh,cbshd->bshd', w, jnp.stack(outs, axis=0).astype(jnp.float32))
    return out.astype(q.dtype)


def forgetting_attention(q, k, v, log_f):
    B, S, H, dh = q.shape
    nb = S // Q_BLOCK
    scale = dh ** -0.5
    c = jnp.cumsum(log_f, axis=1).transpose(0, 2, 1)
    qb = q.reshape(B, nb, Q_BLOCK, H, dh).transpose(1, 0, 2, 3, 4)
    cb = c.reshape(B, H, nb, Q_BLOCK).transpose(2, 0, 1, 3)
    kpos = jnp.arange(S)

    def one_block(args):
        qblk, cq, b_idx = args
        s = jnp.einsum('bqhd,bkhd->bhqk', qblk, k).astype(jnp.float32) * scale
        s = s + cq[..., None] - c[:, :, None, :]
        qpos = b_idx * Q_BLOCK + jnp.arange(Q_BLOCK)
        s = jnp.where(kpos[None, :] <= qpos[:, None], s, -jnp.inf)
        pr = jax.nn.softmax(s, axis=-1)
        return jnp.einsum('bhqk,bkhd->bqhd', pr.astype(v.dtype), v)

    out = lax.map(one_block, (qb, cb, jnp.arange(nb)))
    return out.transpose(1, 0, 2, 3, 4).reshape(B, S, H, dh)


def sb_dilated_mixer(h, positions, w_in, w_out):
    B, S, D = h.shape
    qkv = (h @ w_in).reshape(B, S, 3, N_HEADS, HEAD_DIM)
    q, k, v = qkv[:, :, 0], qkv[:, :, 1], qkv[:, :, 2]
    sb = stick_breaking_attention(q[:, :, :N_SB_HEADS], k[:, :, :N_SB_HEADS], v[:, :, :N_SB_HEADS])
    qd = partial_rope(q[:, :, N_SB_HEADS:], positions)
    kd = partial_rope(k[:, :, N_SB_HEADS:], positions)
    dl = dilated_attention(qd, kd, v[:, :, N_SB_HEADS:])
    o = jnp.concatenate([sb, dl], axis=2).reshape(B, S, D)
    return o @ w_out


def fox_mixer(h, w_in, b_forget, w_out):
    B, S, D = h.shape
    proj = h @ w_in
    qkv = proj[..., :3 * D].reshape(B, S, 3, N_FOX_HEADS, HEAD_DIM)
    log_f = jax.nn.log_sigmoid(proj[..., 3 * D:].astype(jnp.float32) + b_forget.astype(jnp.float32))
    o = forgetting_attention(qkv[:, :, 0], qkv[:, :, 1], qkv[:, :, 2], log_f)
    return o.reshape(B, S, D) @ w_out


def conv_ffn(h, w_up, conv_w, conv_b, w_down):
    u = h @ w_up
    C = u.shape[-1]
    u = lax.conv_general_dilated(u, conv_w[:, None, :], window_strides=(1,),
                                 padding=((CONV_WIDTH - 1, 0),),
                                 dimension_numbers=('NWC', 'WIO', 'NWC'),
                                 feature_group_count=C) + conv_b
    gate, val = jnp.split(u, 2, axis=-1)
    return (jax.nn.silu(gate) * val) @ w_down


def setup_inputs(seed: int = 0) -> dict:
    key = jax.random.key(seed)
    ks = jax.random.split(key, 20)
    D, F = D_MODEL, D_FF
    n_even = (DEPTH + 1) // 2
    n_odd = DEPTH // 2
    f32 = jnp.float32

    def normal(k, shape, fan_in):
        return jax.random.normal(k, shape, f32) * (fan_in ** -0.5)

    def gain(k, shape):
        return 1.0 + 0.02 * jax.random.normal(k, shape, f32)

    x = jax.random.normal(ks[0], (BATCH, SEQ, D), f32)
    p = jax.random.normal(ks[1], (DEPTH, BATCH, SEQ, PLE_DIM), f32)
    offsets = jax.random.randint(ks[2], (BATCH, 1), 0, 4096, dtype=jnp.int32)
    positions = offsets + jnp.arange(SEQ, dtype=jnp.int32)[None, :]
    return {
        "x": x,
        "p": p,
        "positions": positions,
        "attn_norm": gain(ks[3], (DEPTH, D)),
        "w_in_even": normal(ks[4], (n_even, D, 3 * D), D),
        "w_out_even": normal(ks[5], (n_even, D, D), D),
        "w_in_odd": normal(ks[6], (n_odd, D, 3 * D + N_FOX_HEADS), D),
        "b_forget": jax.random.uniform(ks[7], (n_odd, N_FOX_HEADS), f32, 1.0, 6.0),
        "w_out_odd": normal(ks[8], (n_odd, D, D), D),
        "ffn_norm": gain(ks[9], (DEPTH, D)),
        "w_up": normal(ks[10], (DEPTH, D, 2 * F), D),
        "conv_w": normal(ks[11], (DEPTH, CONV_WIDTH, 2 * F), CONV_WIDTH),
        "conv_b": 0.02 * jax.random.normal(ks[12], (DEPTH, 2 * F), f32),
        "w_down": normal(ks[13], (DEPTH, F, D), F),
        "ple_norm": gain(ks[14], (DEPTH, D)),
        "w_ple_gate": normal(ks[15], (DEPTH, D, D), D),
        "w_ple": normal(ks[16], (DEPTH, PLE_DIM, D), PLE_DIM),
        "final_norm": gain(ks[17], (D,)),
    }


def reference(x, p, positions, attn_norm, w_in_even, w_out_even, w_in_odd, b_forget,
              w_out_odd, ffn_norm, w_up, conv_w, conv_b, w_down, ple_norm, w_ple_gate,
              w_ple, final_norm):
    h = x
    for i in range(DEPTH):
        a = rmsnorm(h, attn_norm[i])
        if i % 2 == 0:
            h = h + sb_dilated_mixer(a, positions, w_in_even[i // 2], w_out_even[i // 2])
        else:
            h = h + fox_mixer(a, w_in_odd[i // 2], b_forget[i // 2], w_out_odd[i // 2])
        h = h + conv_ffn(rmsnorm(h, ffn_norm[i]), w_up[i], conv_w[i], conv_b[i], w_down[i])
        gate = jax.nn.sigmoid(rmsnorm(h, ple_norm[i]) @ w_ple_gate[i])
        h = h + gate * (p[i] @ w_ple[i])
    return rmsnorm(h, final_norm)
```

```python
import math
from contextlib import ExitStack

import numpy as np
import concourse.bass as bass
import concourse.mybir as mybir
from concourse.bass_utils import run_bass_kernel_spmd

F32 = mybir.dt.float32
BF16 = mybir.dt.bfloat16
I32 = mybir.dt.int32
ALU = mybir.AluOpType
AF = mybir.ActivationFunctionType

D = 2048
SEQ = 4096
NB = 2
DEPTH = 4
HD = 128
NH = 16
FF = 5632
PLE = 256
T = 1024
NCHUNK = SEQ // T
KC = D // 128
EPS = 1e-6
QSCALE = HD ** -0.5
TWO_PI = 2.0 * math.pi

COMPUTE = ("pe", "act", "dve", "pool")
NROT = 8


class Chan:
    def __init__(self, name):
        self.name = name
        self.n = 0


class Prog:
    def __init__(self):
        self.ins = {e: [] for e in ("pe", "act", "dve", "pool", "sp")}
        self.last_writer = {}
        self.readers = {}
        self.seen = {e: {} for e in self.ins}
        self.chans = []
        self.persist = set()
        self.persist_chans = set()

    def chan(self, name):
        c = Chan(name)
        self.chans.append(c)
        return c

    def _need(self, eng, pid, rec):
        stream, idx = pid
        s = self.seen[eng]
        if s.get(stream, -1) >= idx:
            return
        s[stream] = idx
        rec["waits"].append(pid)
        if isinstance(stream, str):
            self.ins[stream][idx]["marked"] = True

    def op(self, eng, fn, reads=(), writes=(), chan=None):
        lst = self.ins[eng]
        rec = {"fn": fn, "waits": [], "marked": False, "chan": chan}
        is_dma = chan is not None
        if is_dma:
            pid = (chan, chan.n)
            chan.n += 1
        else:
            pid = (eng, len(lst))
        deps = []
        for k in reads:
            w = self.last_writer.get(k)
            if w is not None and not ((not is_dma) and w[0] == eng and eng == "pe"):
                deps.append(w)
        for k in writes:
            w = self.last_writer.get(k)
            if w is not None and not ((not is_dma) and w[0] == eng):
                deps.append(w)
            for r in self.readers.get(k, ()):
                if (not is_dma) and r[0] == eng:
                    continue
                deps.append(r)
        for d in deps:
            self._need(eng, d, rec)
        for k in writes:
            self.last_writer[k] = pid
            self.readers[k] = []
        for k in reads:
            self.readers.setdefault(k, []).append(pid)
        lst.append(rec)
        return pid

    def wait_for(self, eng, pids):
        rec = {"fn": None, "waits": [], "marked": False, "chan": None}
        for p in pids:
            self._need(eng, p, rec)
        self.ins[eng].append(rec)

    def barrier(self):
        pids = []
        for e in COMPUTE:
            for i in range(len(self.ins[e]) - 1, -1, -1):
                if self.ins[e][i]["fn"] is not None and self.ins[e][i]["chan"] is None:
                    pids.append((e, i))
                    break
        for c in self.chans:
            if c.n and c not in self.persist_chans:
                pids.append((c, c.n - 1))
        for e in self.ins:
            if e == "pool":
                continue
            self.wait_for(e, pids)
        self.last_writer = {k: v for k, v in self.last_writer.items() if k in self.persist}
        self.readers = {k: v for k, v in self.readers.items() if k in self.persist}


def emit_program(nc, prog, sems, chan_sems):
    ordinal = {}
    for e in COMPUTE:
        m = 0
        for i, rec in enumerate(prog.ins[e]):
            if rec["marked"]:
                ordinal[(e, i)] = m
                m += 1

    def run(eng_name, eng):
        for i, rec in enumerate(prog.ins[eng_name]):
            for (stream, idx) in rec["waits"]:
                if isinstance(stream, str):
                    m = ordinal[(stream, idx)]
                    eng.wait_ge(sems[stream][m % NROT], m // NROT + 1)
                else:
                    eng.wait_ge(chan_sems[stream], 16 * (idx + 1))
            if rec["fn"] is None:
                continue
            ins = rec["fn"](eng)
            if rec["chan"] is not None:
                ins.then_inc(chan_sems[rec["chan"]], 16)
            elif rec["marked"]:
                m = ordinal[(eng_name, i)]
                ins.then_inc(sems[eng_name][m % NROT], 1)

    with nc.Block() as block:
        @block.tensor
        def _(e):
            run("pe", e)

        @block.scalar
        def _(e):
            run("act", e)

        @block.vector
        def _(e):
            run("dve", e)

        @block.gpsimd
        def _(e):
            run("pool", e)

        @block.sync
        def _(e):
            run("sp", e)


class Arena:
    def __init__(self, nc, base=16512, limit=229376):
        self.nc = nc
        self.off = base
        self.limit = limit
        self.n = 0
        self.peak = base

    def alloc(self, name, shape, dtype):
        isz = 4 if dtype in (F32, I32) else 2
        size = isz
        for s in shape[1:]:
            size *= s
        size = (size + 63) // 64 * 64
        if self.off + size > self.limit:
            raise RuntimeError(f"SBUF arena overflow allocating {name} {shape}: off={self.off} size={size}")
        self.n += 1
        t = self.nc.alloc_sbuf_tensor_at(f"{name}_{self.n}", list(shape), dtype, offset=self.off)
        self.off += size
        self.peak = max(self.peak, self.off)
        return t

    def mark(self):
        return self.off

    def reset(self, m):
        self.off = m


C_IDENT, C_ONES, C_UINCL, C_PM, C_INVF, C_SHC, C_SHS, C_NF = 0, 128, 256, 384, 416, 417, 418, 419
B_ONES, B_MASKF, B_MASKS, B_MASKA, B_USTR, B_LINC, B_NB = 0, 128, 256, 384, 512, 640, 768


def make_consts():
    cf = np.zeros((128, C_NF), np.float32)
    idx = np.arange(128)
    cf[:, C_IDENT:C_IDENT + 128] = np.eye(128, dtype=np.float32)
    cf[:, C_ONES:C_ONES + 128] = 1.0
    cf[:, C_UINCL:C_UINCL + 128] = (idx[:, None] <= idx[None, :]).astype(np.float32)
    pm = np.zeros((32, 32), np.float32)
    for m in range(32):
        pm[(m + 16) % 32, m] = 1.0
    cf[:32, C_PM:C_PM + 32] = pm
    inv_freq = (np.float32(500000.0) ** (-np.arange(0, 32, 2, dtype=np.float32) / np.float32(32.0))).astype(np.float32)
    cf[:32, C_INVF] = np.concatenate([inv_freq, inv_freq])
    cf[:32, C_SHC] = 0.5 * math.pi
    cf[:16, C_SHS] = math.pi
    cf[16:32, C_SHS] = 0.0
    cb = np.zeros((128, B_NB), np.float32)
    cb[:, B_ONES:B_ONES + 128] = 1.0
    cb[:, B_MASKF:B_MASKF + 128] = (idx[:, None] <= idx[None, :])
    cb[:, B_MASKS:B_MASKS + 128] = (idx[:, None] < idx[None, :])
    cb[:, B_MASKA:B_MASKA + 128] = (idx[:, None] >= idx[None, :])
    cb[:, B_USTR:B_USTR + 128] = (idx[:, None] > idx[None, :])
    cb[:, B_LINC:B_LINC + 128] = (idx[:, None] <= idx[None, :])
    return cf, cb


MMCOUNT = [0]
PHASES = []


def _ph(name):
    PHASES.append((name, MMCOUNT[0]))


def build_program(n_layers=DEPTH, n_chunks=NCHUNK, do_mix=True, do_ffn=True, do_ple=True):
    nc = bass.Bass("TRN2", target_bir_lowering=False)
    P = Prog()
    MMCOUNT[0] = 0
    del PHASES[:]

    def din(name, shape, dt=F32):
        return nc.dram_tensor(name, list(shape), dt, kind="ExternalInput").ap()

    xT = din("xT", [D, SEQ])
    pT = din("pT", [DEPTH, PLE, SEQ])
    pos = din("pos", [1, SEQ], I32)
    gains_d = din("gains", [128, 13 * KC])
    cw_d = din("cw", [128, DEPTH * 4 * 88])
    bf_d = din("bfg", [1, 32])
    cf_d = din("cf", [128, C_NF])
    cb_d = din("cb", [128, B_NB])
    w_in_even = din("w_in_even", [2, D, 3 * D])
    w_out_even = din("w_out_even", [2, D, D])
    w_in_odd = din("w_in_odd", [2, D, 3 * D + NH])
    w_out_odd = din("w_out_odd", [2, D, D])
    w_up = din("w_up", [DEPTH, D, 2 * FF])
    w_down = din("w_down", [DEPTH, FF, D])
    w_pg = din("w_ple_gate", [DEPTH, D, D])
    w_pl = din("w_ple", [DEPTH, PLE, D])
    yT = nc.dram_tensor("yT", [D, SEQ], F32, kind="ExternalOutput").ap()
    kT_s = nc.dram_tensor("kT_s", [DEPTH, NH, 128, SEQ], BF16).ap()
    v_s = nc.dram_tensor("v_s", [DEPTH, NH, SEQ, 128], BF16).ap()

    A = Arena(nc)
    hT = A.alloc("hT", [128, KC, T], F32)
    aT = A.alloc("aT", [128, KC, T], BF16)
    X = A.alloc("X", [128, 32, 512], BF16)
    NWB = 4
    WB = [A.alloc("wb", [128, 16, 256], BF16) for _ in range(NWB)]
    negc = A.alloc("negc", [128, 2, 32, NH], F32)
    ctot = A.alloc("ctot", [128, 2, NH], F32)
    ucar = A.alloc("ucar", [128, DEPTH, 88, 2], F32)
    gains = A.alloc("gains", [128, 13, KC], F32)
    cw = A.alloc("cw", [128, DEPTH, 4, 88], F32)
    cf = A.alloc("cf", [128, C_NF], F32)
    cb = A.alloc("cb", [128, B_NB], BF16)
    bfb = A.alloc("bfb", [128, 32], F32)
    persist_mark = A.mark()

    for i_ in range(NWB):
        P.persist.add(("wb", i_))
    PS = [nc.alloc_psum_tensor(f"ps{i}", [128, 512], F32) for i in range(8)]

    ident = cf[:, C_IDENT:C_IDENT + 128]
    ones_f = cf[:, C_ONES:C_ONES + 128]
    uincl_f = cf[:, C_UINCL:C_UINCL + 128]
    ones_b = cb[:, B_ONES:B_ONES + 128]

    chans = {}

    def chan_of(ckey):
        if ckey not in chans:
            chans[ckey] = P.chan("c%d" % len(chans))
        return chans[ckey]

    def dma(eng, ckey, out, in_, reads, writes):
        return P.op(eng, lambda e: e.dma_start(out=out, in_=in_), reads, writes, chan=chan_of(ckey))

    def mm(out, lhsT, rhs, start, stop, reads, writes):
        MMCOUNT[0] += 1
        return P.op("pe", lambda e: e.matmul(out, lhsT=lhsT, rhs=rhs, start=start, stop=stop, skip_group_check=True), reads, writes)

    def act(out, in_, func, reads, writes, bias=None, scale=None):
        kw = {}
        if bias is not None:
            kw["bias"] = bias
        if scale is not None:
            kw["scale"] = scale
        return P.op("act", lambda e: e.activation(out=out, in_=in_, func=func, **kw), reads, writes)

    def tt_(eng, out, in0, in1, op, reads, writes):
        return P.op(eng, lambda e: e.tensor_tensor(out=out, in0=in0, in1=in1, op=op), reads, writes)

    def ts_(eng, out, in0, s1, s2, op0, op1, reads, writes):
        return P.op(eng, lambda e: e.tensor_scalar(out=out, in0=in0, scalar1=s1, scalar2=s2, op0=op0, op1=op1), reads, writes)

    def stt_(eng, out, in0, scalar, in1, op0, op1, reads, writes):
        return P.op(eng, lambda e: e.scalar_tensor_tensor(out=out, in0=in0, scalar=scalar, in1=in1, op0=op0, op1=op1), reads, writes)

    def cp_(eng, out, in_, reads, writes):
        return P.op(eng, lambda e: e.tensor_copy(out=out, in_=in_), reads, writes)

    def memset_(eng, ap, val, writes):
        return P.op(eng, lambda e: e.memset(ap, val), (), writes)

    dma("sp", "gains", gains[:, :, :], gains_d.rearrange("p (n k) -> p n k", k=KC), [], ["gains"])
    dma("sp", "cw", cw[:, :, :, :], cw_d.rearrange("p (l t f) -> p l t f", l=DEPTH, t=4), [], ["cw"])
    dma("sp", "cf", cf[:, :], cf_d[:, :], [], ["cf"])
    dma("pool", "cb", cb[:, :], cb_d[:, :], [], ["cb"])
    dma("sp", "bfb", bfb[:, :], bf_d[0:1, :].broadcast_to([128, 32]), [], ["bfb"])
    memset_("dve", ctot[:, :, :], 0.0, ["ctot"])
    memset_("dve", ucar[:, :, :, :], 0.0, ["ucar"])

    wcount = [0]

    def load_wblock(src_ap, nk, ncols):
        i = wcount[0] % NWB
        wcount[0] += 1
        blk = WB[i]
        dma("pool", ("wb", i), blk[:, 0:nk, 0:ncols], src_ap.rearrange("(k p) c -> p k c", p=128), [], [("wb", i)])
        P.persist_chans.add(chan_of(("wb", i)))
        return blk, ("wb", i)

    bank_rr = [0]

    def next_bank():
        b = bank_rr[0] % 8
        bank_rr[0] += 1
        return b

    def rmsnorm_phase(gi, final_chunk=None, hold=False):
        m0 = A.mark()
        sq = [A.alloc("sq", [128, 512], BF16) for _ in range(3)]
        rstd = [A.alloc("rstd", [128, 512], F32) for _ in range(2)]
        yst = [A.alloc("yst", [128, 512], F32) for _ in range(3)] if final_chunk is not None else None
        for tt in range(2):
            b = next_bank()
            cs = slice(tt * 512, tt * 512 + 512)
            for kc in range(KC):
                s = sq[kc % 3]
                act(s[:, :], hT[:, kc, cs], AF.Square, [("hT", kc, tt)], [("sq", kc % 3)])
                mm(PS[b][:, :], ones_b, s[:, :], kc == 0, kc == KC - 1, [("sq", kc % 3), "cb"], [("ps", b)])
            r = rstd[tt]
            act(r[:, :], PS[b][:, :], AF.Ln, [("ps", b)], [("rstd", tt)], bias=EPS, scale=1.0 / D)
            act(r[:, :], r[:, :], AF.Exp, [("rstd", tt)], [("rstd", tt)], scale=-0.5)
            for kc in range(KC):
                if final_chunk is None:
                    stt_("dve", aT[:, kc, cs], hT[:, kc, cs], gains[:, gi, kc:kc + 1], r[:, :], ALU.mult, ALU.mult,
                         [("hT", kc, tt), ("rstd", tt), "gains"], [("aT", kc, tt)])
                else:
                    y = yst[kc % 3]
                    stt_("dve", y[:, :], hT[:, kc, cs], gains[:, gi, kc:kc + 1], r[:, :], ALU.mult, ALU.mult,
                         [("hT", kc, tt), ("rstd", tt), "gains"], [("yst", kc % 3)])
                    t0 = final_chunk * T + tt * 512
                    pid = dma("sp", ("yst", kc % 3), yT[kc * 128:(kc + 1) * 128, t0:t0 + 512], y[:, :], [("yst", kc % 3)], [])
                    out_pids.append(pid)
        if hold:
            return m0
        P.barrier()
        A.reset(m0)
        return m0

    out_pids = []

    def proj_fm(W2d, ncolblocks, col0, rhs_fn, rhs_keys_fn, nk_total, consume, tts=(0, 1)):
        kgroups = []
        k = 0
        while k < nk_total:
            kgroups.append((k, min(16, nk_total - k)))
            k += 16
        for cbk in range(ncolblocks):
            c0 = col0 + cbk * 256
            banks = {}
            for gi_, (k0, nk) in enumerate(kgroups):
                blk, wkey = load_wblock(W2d[k0 * 128:(k0 + nk) * 128, c0:c0 + 256], nk, 256)
                for sub in range(2):
                    for tt in tts:
                        if gi_ == 0:
                            banks[(sub, tt)] = next_bank()
                        b = banks[(sub, tt)]
                        for kc in range(nk):
                            gk = k0 + kc
                            mm(PS[b][:, :], blk[:, kc, sub * 128:(sub + 1) * 128], rhs_fn(gk, tt),
                               gk == 0, gk == nk_total - 1, [wkey] + rhs_keys_fn(gk, tt), [("ps", b)])
            for sub in range(2):
                for tt in tts:
                    consume((c0 - col0) // 128 + sub, tt, banks[(sub, tt)])

    def aT_rhs(kc, tt):
        return aT[:, kc, tt * 512:(tt + 1) * 512]

    def aT_keys(kc, tt):
        return [("aT", kc, tt)]

    def qo(h, tt):
        return X[:, 2 * h + tt, :]

    def qo_key(h, tt):
        return ("X", 2 * h + tt)

    def qkv_phase(l, j):
        even = (l % 2 == 0)
        W = (w_in_even if even else w_in_odd)[l // 2]
        t0 = j * T
        m0 = A.mark()
        kst = [A.alloc("kst", [128, 512], BF16) for _ in range(2)]
        vst = [A.alloc("vst", [128, 256], BF16) for _ in range(2)]
        cnt = {"k": 0, "v": 0, "r": 0}
        if even:
            rst = [A.alloc("rst", [128, 512], F32) for _ in range(2)]
            rt1 = [A.alloc("rt1", [32, 512], F32) for _ in range(2)]
            rt2 = [A.alloc("rt2", [32, 512], F32) for _ in range(2)]
            ropeC = A.alloc("ropeC", [32, T], F32)
            ropeS = A.alloc("ropeS", [32, T], F32)
            posi = A.alloc("posi", [32, T], I32)
            rtmp = A.alloc("rtmp", [32, T], F32)
            dma("sp", "posi", posi[:, :], pos[0:1, t0:t0 + T].broadcast_to([32, T]), [], ["posi"])
            cp_("dve", ropeC[:, :], posi[:, :], ["posi"], ["ropeC"])
            MAGIC = 12582912.0
            PI_C = 3.14159
            for tab, shc in ((ropeS, C_SHS), (ropeC, C_SHC)):
                nm = "ropeS" if tab is ropeS else "ropeC"
                if tab is ropeS:
                    ts_("dve", ropeS[:, :], ropeC[:, :], cf[0:32, C_INVF:C_INVF + 1], cf[0:32, shc:shc + 1], ALU.mult, ALU.add,
                        ["ropeC", "cf"], ["ropeS"])
                else:
                    ts_("dve", ropeC[:, :], ropeC[:, :], cf[0:32, C_INVF:C_INVF + 1], cf[0:32, shc:shc + 1], ALU.mult, ALU.add,
                        ["ropeC", "cf"], ["ropeC"])
                ts_("dve", rtmp[:, :], tab[:, :], 1.0 / TWO_PI, MAGIC, ALU.mult, ALU.add, [nm], ["rtmp"])
                ts_("dve", rtmp[:, :], rtmp[:, :], -MAGIC, 0.0, ALU.add, ALU.add, ["rtmp"], ["rtmp"])
                stt_("dve", tab[:, :], rtmp[:, :], -TWO_PI, tab[:, :], ALU.mult, ALU.add, ["rtmp", nm], [nm])
                ts_("dve", tab[:, :], tab[:, :], -PI_C, PI_C, ALU.max, ALU.min, [nm], [nm])
                act(tab[:, :], tab[:, :], AF.Sin, [nm], [nm])

        def rope_evac(b, tt, dst_ap, dst_key, scale):
            i = cnt["r"] % 2
            cnt["r"] += 1
            cs = slice(tt * 512, tt * 512 + 512)
            r = rst[i]
            act(r[:, :], PS[b][:, :], AF.Copy, [("ps", b)], [("rst", i)], scale=scale)
            b2 = next_bank()
            mm(PS[b2][0:32, :], cf[0:32, C_PM:C_PM + 32], r[0:32, :], True, True, [("rst", i), "cf"], [("ps", b2)])
            cp_("dve", dst_ap, r[:, :], [("rst", i)], [dst_key])
            tt_("dve", rt1[i][:, :], r[0:32, :], ropeC[:, cs], ALU.mult, [("rst", i), "ropeC"], [("rt1", i)])
            tt_("dve", rt2[i][:, :], PS[b2][0:32, :], ropeS[:, cs], ALU.mult, [("ps", b2), "ropeS"], [("rt2", i)])
            tt_("dve", dst_ap[0:32, :], rt1[i][:, :], rt2[i][:, :], ALU.add, [("rt1", i), ("rt2", i)], [dst_key])

        def consume_q(c128, tt, b):
            h = c128
            if even and h >= 8:
                rope_evac(b, tt, qo(h, tt), qo_key(h, tt), QSCALE)
            else:
                act(qo(h, tt), PS[b][:, :], AF.Copy, [("ps", b)], [qo_key(h, tt)], scale=QSCALE)

        def consume_k(c128, tt, b):
            h = c128
            i = cnt["k"] % 2
            cnt["k"] += 1
            ks = kst[i]
            if even and h >= 8:
                rope_evac(b, tt, ks[:, :], ("kst", i), 1.0)
            else:
                cp_("dve", ks[:, :], PS[b][:, :], [("ps", b)], [("kst", i)])
            dma("sp", ("kst", i), kT_s[l, h, :, t0 + tt * 512:t0 + tt * 512 + 512], ks[:, :], [("kst", i)], [("kTs", l, h, j, tt)])

        proj_fm(W, 8, 0, aT_rhs, aT_keys, KC, consume_q)
        proj_fm(W, 8, D, aT_rhs, aT_keys, KC, consume_k)
        for cbk in range(8):
            c0 = 2 * D + cbk * 256
            blk, wkey = load_wblock(W[:, c0:c0 + 256], KC, 256)
            for mt in range(8):
                b = next_bank()
                for kc in range(KC):
                    mm(PS[b][:, 0:256], aT[:, kc, mt * 128:(mt + 1) * 128], blk[:, kc, :], kc == 0, kc == KC - 1,
                       [wkey, ("aT", kc, mt // 4)], [("ps", b)])
                i = cnt["v"] % 2
                cnt["v"] += 1
                vs = vst[i]
                cp_("dve", vs[:, :], PS[b][:, 0:256], [("ps", b)], [("vst", i)])
                for sub in range(2):
                    h = 2 * cbk + sub
                    dma("sp", ("vst", i, sub), v_s[l, h, t0 + mt * 128:t0 + (mt + 1) * 128, :], vs[:, sub * 128:(sub + 1) * 128],
                        [("vst", i)], [("vs", l, h, j, mt)])
        if not even:
            fl = l // 2
            nlf = [A.alloc("nlf", [128, NH], F32) for _ in range(2)]
            blk, wkey = load_wblock(W[:, 3 * D:3 * D + NH], KC, NH)
            for mt in range(8):
                g = j * 8 + mt
                b = next_bank()
                for kc in range(KC):
                    mm(PS[b][:, 0:NH], aT[:, kc, mt * 128:(mt + 1) * 128], blk[:, kc, 0:NH], kc == 0, kc == KC - 1,
                       [wkey, ("aT", kc, mt // 4)], [("ps", b)])
                n = nlf[mt % 2]
                nk_ = ("nlf", mt % 2)
                tt_("dve", n[:, :], PS[b][:, 0:NH], bfb[:, fl * NH:(fl + 1) * NH], ALU.add, [("ps", b), "bfb"], [nk_])
                act(n[:, :], n[:, :], AF.Exp, [nk_], [nk_], scale=-1.0)
                act(n[:, :], n[:, :], AF.Ln, [nk_], [nk_], bias=1.0)
                b2 = next_bank()
                mm(PS[b2][:, 0:NH], uincl_f, n[:, :], True, True, [nk_, "cf"], [("ps", b2)])
                mm(PS[b2][:, NH:2 * NH], ones_f, n[:, :], True, True, [nk_, "cf"], [("ps", b2)])
                tt_("dve", negc[:, fl, g, :], PS[b2][:, 0:NH], ctot[:, fl, :], ALU.add, [("ps", b2), "ctot"], [("negc", fl, g)])
                tt_("dve", ctot[:, fl, :], PS[b2][:, NH:2 * NH], ctot[:, fl, :], ALU.add, [("ps", b2), "ctot"], ["ctot"])
        P.barrier()
        A.reset(m0)

    def pipeline(steps, offsets):
        n = len(steps)
        ns = len(offsets)
        for t_ in range(n + offsets[-1]):
            for st in range(ns - 1, -1, -1):
                i_ = t_ - offsets[st]
                if 0 <= i_ < n:
                    steps[i_][st]()

    def load_kv(l, h, c, kbuf, vbuf, i):
        dma("sp", ("kbuf", i), kbuf[i][:, :], kT_s[l, h, :, c * T:(c + 1) * T], [("kTs", l, h, c, tt) for tt in range(2)], [("kbuf", i)])
        dma("sp", ("vbuf", i), vbuf[i][:, :, :], v_s[l, h, c * T:(c + 1) * T, :].rearrange("(m p) d -> p m d", p=128),
            [("vs", l, h, c, mt) for mt in range(8)], [("vbuf", i)])

    def fox_attention(l, j):
        fl = l // 2
        m0 = A.mark()
        kbuf = [A.alloc("kbuf", [128, T], BF16) for _ in range(2)]
        vbuf = [A.alloc("vbuf", [128, 8, 128], BF16) for _ in range(2)]
        NBF = 4
        PT = [A.alloc("PT", [128, 512], BF16) for _ in range(NBF)]
        tmp = [A.alloc("tmpF", [128, 512], F32) for _ in range(3)]
        cqb = [A.alloc("cqb", [128, T], F32) for _ in range(2)]
        dg = [A.alloc("dg", [128, 128], F32) for _ in range(2)]
        rinv = [A.alloc("rinv", [128, 512], F32) for _ in range(2)]
        ZB = [0, 1, 2]
        OB = [3, 4]
        RB = [5, 6]
        CQ = 7
        maskf = cb[:, B_MASKF:B_MASKF + 128]
        cnt = {"load": 0, "step": 0}

        def prep_cqb(h):
            cq = cqb[h % 2]
            for half in range(2):
                for q4 in range(4):
                    qb = half * 4 + q4
                    g = j * 8 + qb
                    d_ = dg[qb % 2]
                    ts_("dve", d_[:, :], ident, negc[:, fl, g, h:h + 1], -1.0, ALU.mult, ALU.mult,
                        ["cf", ("negc", fl, g)], [("dg", qb % 2)])
                    mm(PS[CQ][:, q4 * 128:(q4 + 1) * 128], ones_f, d_[:, :], True, True, [("dg", qb % 2), "cf"], [("ps", CQ)])
                act(cq[:, half * 512:(half + 1) * 512], PS[CQ][:, :], AF.Copy, [("ps", CQ)], [("cqb", h % 2, half)])

        prep_cqb(0)
        segs = [(h_, c_) for h_ in range(NH) for c_ in range(j + 1)]
        load_kv(l, segs[0][0], segs[0][1], kbuf, vbuf, 0)
        for h in range(NH):
            cq = cqb[h % 2]
            steps = []
            for c in range(j + 1):
                i = cnt["load"] % 2
                nxt = cnt["load"] + 1
                cnt["load"] += 1
                ld = [0]
                for qt in range(2):
                    for ktl in range(8):
                        rel = None
                        if c == j:
                            if ktl > 4 * qt + 3:
                                continue
                            if ktl >= 4 * qt:
                                rel = ktl - 4 * qt
                        off = 0 if rel is None else rel * 128
                        first = (c == 0 and ktl == 0)
                        last = (c == j and ktl == 4 * qt + 3)
                        g = c * 8 + ktl
                        sk = cnt["step"] % NBF
                        zb = ZB[cnt["step"] % 3]
                        cnt["step"] += 1
                        ncol = 512 - off
                        do_load = (ld[0] == 4)
                        ld[0] += 1

                        def s1(i=i, c=c, ktl=ktl, qt=qt, off=off, ncol=ncol, zb=zb, do_load=do_load, h=h, nxt=nxt):
                            if do_load and nxt < len(segs):
                                load_kv(l, segs[nxt][0], segs[nxt][1], kbuf, vbuf, nxt % 2)
                            mm(PS[zb][:, 0:ncol], kbuf[i][:, ktl * 128:(ktl + 1) * 128], qo(h, qt)[:, off:512], True, True,
                               [("kbuf", i), qo_key(h, qt)], [("ps", zb)])

                        def s2a(qt=qt, off=off, ncol=ncol, zb=zb, sk=sk, g=g, h=h, cq=cq):
                            stt_("dve", tmp[sk % 3][:, 0:ncol], PS[zb][:, 0:ncol], negc[:, fl, g, h:h + 1], cq[:, qt * 512 + off:(qt + 1) * 512],
                                 ALU.add, ALU.add, [("ps", zb), ("negc", fl, g), ("cqb", h % 2, qt)], [("tmp", sk % 3)])

                        def s2(ncol=ncol, sk=sk):
                            act(PT[sk][:, 0:ncol], tmp[sk % 3][:, 0:ncol], AF.Exp, [("tmp", sk % 3)], [("PT", sk)])

                        def s2m(sk=sk, rel=rel):
                            if rel is not None:
                                tt_("dve", PT[sk][:, 0:128], PT[sk][:, 0:128], maskf, ALU.mult, [("PT", sk), "cb"], [("PT", sk)])

                        def s3(i=i, ktl=ktl, qt=qt, off=off, ncol=ncol, sk=sk, first=first, last=last):
                            mm(PS[OB[qt]][:, off:512], vbuf[i][:, ktl, :], PT[sk][:, 0:ncol], first, last, [("vbuf", i), ("PT", sk)], [("ps", OB[qt])])
                            mm(PS[RB[qt]][:, off:512], ones_b, PT[sk][:, 0:ncol], first, last, [("PT", sk), "cb"], [("ps", RB[qt])])

                        steps.append((s1, s2a, s2, s2m, s3))
            nst = len(steps)
            if h + 1 < NH:
                mid = min(6, nst - 1)
                s1_old = steps[mid][0]
                steps[mid] = ((lambda f=s1_old, hh=h + 1: (prep_cqb(hh), f())),) + tuple(steps[mid][1:])
            def evac(qt, h=h):
                act(rinv[qt][:, :], PS[RB[qt]][:, :], AF.Ln, [("ps", RB[qt])], [("rinv", qt)])
                act(rinv[qt][:, :], rinv[qt][:, :], AF.Exp, [("rinv", qt)], [("rinv", qt)], scale=-1.0)
                tt_("dve", qo(h, qt), PS[OB[qt]][:, :], rinv[qt][:, :], ALU.mult, [("ps", OB[qt]), ("rinv", qt)], [qo_key(h, qt)])

            li = nst - 9
            if li + 2 < nst:
                tgt = li + 2
                p2_old = steps[tgt][4]
                steps[tgt] = tuple(steps[tgt][:4]) + ((lambda f=p2_old: (f(), evac(0))),)
                pipeline(steps, [0, 1, 2, 3, 5])
                evac(1)
            else:
                pipeline(steps, [0, 1, 2, 3, 5])
                evac(0)
                evac(1)
        P.barrier()
        A.reset(m0)

    def sb_attention(l, j, heads):
        m0 = A.mark()
        kbuf = [A.alloc("kbuf", [128, T], BF16) for _ in range(2)]
        vbuf = [A.alloc("vbuf", [128, 8, 128], BF16) for _ in range(2)]
        NS = 3
        PT = [A.alloc("PT", [128, 512], BF16) for _ in range(NS)]
        LB = [A.alloc("LB", [128, 512], BF16) for _ in range(NS)]
        tES = [A.alloc("tES", [128, 512], F32) for _ in range(2)]
        tT = [A.alloc("tT", [128, 512], F32) for _ in range(2)]
        Cc = [A.alloc("Cc", [128, 512], F32) for _ in range(2)]
        ZB = [0, 1, 2, 3]
        BB = [4, 5]
        OB = [6, 7]
        uge = cb[:, B_MASKA:B_MASKA + 128]
        masks = cb[:, B_MASKS:B_MASKS + 128]
        cnt = {"load": 0, "step": 0}
        segs = [(h_, c_) for h_ in heads for c_ in range(j, -1, -1)]
        load_kv(l, segs[0][0], segs[0][1], kbuf, vbuf, 0)
        for h in heads:
            steps = []
            nsteps = {0: 0, 1: 0}
            total = {qt: sum(1 for c in range(j + 1) for ktl in range(8) if not (c == j and ktl > 4 * qt + 3)) for qt in range(2)}
            for qt in range(2):
                memset_("dve", Cc[qt][:, :], 0.0, [("Cc", qt)])
            for c in range(j, -1, -1):
                i = cnt["load"] % 2
                nxt = cnt["load"] + 1
                cnt["load"] += 1
                ld = [0]
                for ktl in range(7, -1, -1):
                    for qt in range(2):
                        rel = None
                        if c == j:
                            if ktl > 4 * qt + 3:
                                continue
                            if ktl >= 4 * qt:
                                rel = ktl - 4 * qt
                        off = 0 if rel is None else rel * 128
                        ncol = 512 - off
                        first = nsteps[qt] == 0
                        nsteps[qt] += 1
                        last = nsteps[qt] == total[qt]
                        n_ = cnt["step"]
                        cnt["step"] += 1
                        zb = ZB[n_ % 4]
                        bb = BB[n_ % 2]
                        s3i = n_ % NS
                        s2i = n_ % 2
                        do_load = (ld[0] == 6)
                        ld[0] += 1

                        def s1(i=i, c=c, ktl=ktl, qt=qt, off=off, ncol=ncol, zb=zb, do_load=do_load, h=h, nxt=nxt):
                            if do_load and nxt < len(segs):
                                load_kv(l, segs[nxt][0], segs[nxt][1], kbuf, vbuf, nxt % 2)
                            mm(PS[zb][:, 0:ncol], kbuf[i][:, ktl * 128:(ktl + 1) * 128], qo(h, qt)[:, off:512], True, False,
                               [("kbuf", i), qo_key(h, qt)], [("ps", zb)])

                        def s2(ncol=ncol, zb=zb, s2i=s2i):
                            act(tES[s2i][:, 0:ncol], PS[zb][:, 0:ncol], AF.Exp, [("ps", zb)], [("tES", s2i)])
                            act(tES[s2i][:, 0:ncol], tES[s2i][:, 0:ncol], AF.Ln, [("tES", s2i)], [("tES", s2i)], bias=1.0)

                        def s2b(ncol=ncol, s2i=s2i, s3i=s3i, rel=rel):
                            ts_("dve", LB[s3i][:, 0:ncol], tES[s2i][:, 0:ncol], -1.0, 0.0, ALU.mult, ALU.add, [("tES", s2i)], [("LB", s3i)])
                            if rel is not None:
                                tt_("dve", LB[s3i][:, 0:128], LB[s3i][:, 0:128], masks, ALU.mult, [("LB", s3i), "cb"], [("LB", s3i)])

                        def s3(ncol=ncol, zb=zb, bb=bb, s3i=s3i):
                            mm(PS[zb][:, 0:ncol], uge, LB[s3i][:, 0:ncol], False, True, [("LB", s3i), "cb"], [("ps", zb)])
                            mm(PS[bb][:, 0:ncol], ones_b, LB[s3i][:, 0:ncol], True, True, [("LB", s3i), "cb"], [("ps", bb)])

                        def s4(qt=qt, off=off, ncol=ncol, zb=zb, bb=bb, s2i=s2i):
                            tt_("dve", tT[s2i][:, 0:ncol], PS[zb][:, 0:ncol], Cc[qt][:, off:512], ALU.add, [("ps", zb), ("Cc", qt)], [("tT", s2i)])
                            tt_("dve", Cc[qt][:, off:512], Cc[qt][:, off:512], PS[bb][:, 0:ncol], ALU.add, [("ps", bb), ("Cc", qt)], [("Cc", qt)])

                        def s4b(ncol=ncol, s2i=s2i, s3i=s3i, rel=rel):
                            act(PT[s3i][:, 0:ncol], tT[s2i][:, 0:ncol], AF.Exp, [("tT", s2i)], [("PT", s3i)])
                            if rel is not None:
                                tt_("dve", PT[s3i][:, 0:128], PT[s3i][:, 0:128], masks, ALU.mult, [("PT", s3i), "cb"], [("PT", s3i)])

                        def s5(i=i, ktl=ktl, qt=qt, off=off, ncol=ncol, s3i=s3i, first=first, last=last):
                            mm(PS[OB[qt]][:, off:512], vbuf[i][:, ktl, :], PT[s3i][:, 0:ncol], first, last, [("vbuf", i), ("PT", s3i)], [("ps", OB[qt])])

                        steps.append((s1, s2, s2b, s3, s4, s4b, s5))
            pipeline(steps, [0, 1, 2, 3, 4, 5, 6])
            for qt in range(2):
                act(qo(h, qt), PS[OB[qt]][:, :], AF.Copy, [("ps", OB[qt])], [qo_key(h, qt)])
        P.barrier()
        A.reset(m0)

    def dil_attention(l, j, heads):
        m0 = A.mark()
        t0 = j * T
        lo_chunk = max(0, j - 2)
        nwin = (j - lo_chunk + 1) * T
        wbase = lo_chunk * T
        kwin = [A.alloc("kwin", [128, 3 * T], BF16) for _ in range(2)]
        NV = 6
        NP = 4
        vt = [A.alloc("vt", [128, 128], BF16) for _ in range(NV)]
        PT = [A.alloc("PTd", [128, 128], BF16) for _ in range(NP)]
        rinv = [A.alloc("rinv", [128, 512], F32) for _ in range(2)]
        SB_ = [0, 1, 2]
        OB = [3, 4]
        RB = [5, 6]
        maskf = cb[:, B_MASKF:B_MASKF + 128]
        maska = cb[:, B_MASKA:B_MASKA + 128]
        cnt = {"step": 0, "nv": 0}

        def load_kwin(hh, wi):
            dma("sp", ("kwin", wi), kwin[wi][:, 0:nwin], kT_s[l, hh, :, wbase:wbase + nwin],
                [("kTs", l, hh, c, tt) for c in range(lo_chunk, j + 1) for tt in range(2)], [("kwin", wi)])

        load_kwin(heads[0], 0)
        for hi, h in enumerate(heads):
            wi = hi % 2
            if hi + 1 < len(heads):
                load_kwin(heads[hi + 1], (hi + 1) % 2)
            started = {0: False, 1: False}
            steps = []
            for (dil, nq) in ((1, 128), (4, 128), (16, 64)):
                L = T // dil
                for r in range(dil):
                    for mb in range(L // nq):
                        m_loc = mb * nq
                        M0 = (t0 // dil) + m_loc
                        tiles = []
                        nka = min(128, M0)
                        if nka > 0:
                            tiles.append((M0 - nka, nka, "A" if nka == 128 else None))
                        tiles.append((M0, nq, "F"))
                        for (K0, nk, mk) in tiles:
                            ktok0 = r + dil * K0
                            qtok0 = r + dil * m_loc
                            kcol0 = ktok0 - wbase
                            n_ = cnt["step"]
                            cnt["step"] += 1
                            sb = SB_[n_ % 3]
                            sk = n_ % NP
                            vi = n_ % NV
                            hq, tq = divmod(qtok0, 512)
                            if dil == 16:
                                pieces = [(0, r, 0, 32), (1, r, 32, 64)]
                            else:
                                pieces = [(hq, tq, 0, nq)]
                            flags = []
                            for (tt, _, _, _) in pieces:
                                flags.append(not started[tt])
                                started[tt] = True

                            def s1(dil=dil, r=r, nq=nq, nk=nk, ktok0=ktok0, kcol0=kcol0, sb=sb, vi=vi, hq=hq, tq=tq, h=h, wi=wi):
                                vsrc = v_s[l, h, ktok0:ktok0 + dil * (nk - 1) + 1:dil, :]
                                kc_set = sorted(set([(ktok0) // T, (ktok0 + dil * (nk - 1)) // T]))
                                dma("sp", ("vt", vi), vt[vi][0:nk, :], vsrc, [("vs", l, h, c, mt) for c in kc_set for mt in range(8)], [("vt", vi)])
                                kl = kwin[wi][:, kcol0:kcol0 + dil * (nk - 1) + 1:dil]
                                if dil == 16:
                                    for half in range(2):
                                        qs = qo(h, half)[:, r:r + 16 * 31 + 1:16]
                                        mm(PS[sb][0:nk, half * 32:(half + 1) * 32], kl, qs, True, True, [("kwin", wi), qo_key(h, half)], [("ps", sb)])
                                else:
                                    qs = qo(h, hq)[:, tq:tq + dil * (nq - 1) + 1:dil]
                                    mm(PS[sb][0:nk, 0:nq], kl, qs, True, True, [("kwin", wi), qo_key(h, hq)], [("ps", sb)])

                            def s2a(nq=nq, nk=nk, sb=sb, sk=sk):
                                act(PT[sk][0:nk, 0:nq], PS[sb][0:nk, 0:nq], AF.Exp, [("ps", sb)], [("PTd", sk)])

                            def s2(nq=nq, nk=nk, sk=sk, mk=mk):
                                pt = PT[sk]
                                if mk == "F":
                                    tt_("dve", pt[0:nk, 0:nq], pt[0:nk, 0:nq], maskf[0:nk, 0:nq], ALU.mult, [("PTd", sk), "cb"], [("PTd", sk)])
                                elif mk == "A":
                                    tt_("dve", pt[0:nk, 0:nq], pt[0:nk, 0:nq], maska[0:nk, 0:nq], ALU.mult, [("PTd", sk), "cb"], [("PTd", sk)])

                            def s3(dil=dil, nk=nk, sk=sk, vi=vi, pieces=pieces, flags=flags):
                                pt = PT[sk]
                                for (tt, tq_, a_, b_), first in zip(pieces, flags):
                                    nn = b_ - a_
                                    osl = slice(tq_, tq_ + dil * (nn - 1) + 1, dil)
                                    mm(PS[OB[tt]][:, osl], vt[vi][0:nk, :], pt[0:nk, a_:b_], first, False, [("vt", vi), ("PTd", sk)], [("ps", OB[tt])])
                                    mm(PS[RB[tt]][:, osl], ones_b[0:nk, :], pt[0:nk, a_:b_], first, False, [("PTd", sk), "cb"], [("ps", RB[tt])])

                            steps.append((s1, s2a, s2, s3))
            pipeline(steps, [0, 1, 2, 4])
            for qt in range(2):
                act(rinv[qt][:, :], PS[RB[qt]][:, :], AF.Ln, [("ps", RB[qt])], [("rinv", qt)])
                act(rinv[qt][:, :], rinv[qt][:, :], AF.Exp, [("rinv", qt)], [("rinv", qt)], scale=-1.0)
                tt_("dve", qo(h, qt), PS[OB[qt]][:, :], rinv[qt][:, :], ALU.mult, [("ps", OB[qt]), ("rinv", qt)], [qo_key(h, qt)])
        P.barrier()
        A.reset(m0)

    def wout_phase(l):
        W = (w_out_even if l % 2 == 0 else w_out_odd)[l // 2]

        def consume(c128, tt, b):
            cs = slice(tt * 512, tt * 512 + 512)
            tt_("dve", hT[:, c128, cs], hT[:, c128, cs], PS[b][:, :], ALU.add, [("hT", c128, tt), ("ps", b)], [("hT", c128, tt)])

        proj_fm(W, 8, 0, lambda kc, tt: qo(kc, tt), lambda kc, tt: [qo_key(kc, tt)], KC, consume)

    def ffn_phase(l, j):
        Wu = w_up[l]
        Wd = w_down[l]
        m0 = A.mark()
        ub = {(s, tt): A.alloc("ub", [128, 514], F32) for s in range(2) for tt in range(2)}
        acc = [A.alloc("acc", [128, 512], F32) for _ in range(4)]
        nacc = 0
        for fq in range(4):
            blocks = [(fq * 11 + 2 * i_, 2) for i_ in range(5)] + [(fq * 11 + 10, 1)]
            for (fc0, nsub) in blocks:
                gblk, gkey = load_wblock(Wu[:, fc0 * 128:fc0 * 128 + nsub * 128], KC, nsub * 128)
                vblk, vkey = load_wblock(Wu[:, FF + fc0 * 128:FF + fc0 * 128 + nsub * 128], KC, nsub * 128)
                for sub in range(nsub):
                    fc = fc0 + sub
                    fcl = fc - fq * 11
                    for tt in range(2):
                        res = []
                        for s, (blk, wkey) in enumerate(((gblk, gkey), (vblk, vkey))):
                            b = next_bank()
                            for kc in range(KC):
                                mm(PS[b][:, :], blk[:, kc, sub * 128:(sub + 1) * 128], aT_rhs(kc, tt), kc == 0, kc == KC - 1,
                                   [wkey, ("aT", kc, tt)], [("ps", b)])
                            ch = fc + 44 * s
                            u = ub[(s, tt)]
                            uk = ("ub", s, tt)
                            act(u[:, 2:514], PS[b][:, :], AF.Copy, [("ps", b)], [uk])
                            if tt == 0:
                                cp_("dve", u[:, 0:2], ucar[:, l, ch, :], ["ucar"], [uk])
                            else:
                                cp_("dve", u[:, 0:2], ub[(s, 0)][:, 512:514], [("ub", s, 0)], [uk])
                                cp_("dve", ucar[:, l, ch, :], u[:, 512:514], [uk], ["ucar"])
                            a_ = acc[nacc % 4]
                            ak = ("acc", nacc % 4)
                            nacc += 1
                            ts_("dve", a_[:, :], u[:, 2:514], cw[:, l, 2, ch:ch + 1], cw[:, l, 3, ch:ch + 1], ALU.mult, ALU.add, [uk, "cw"], [ak])
                            stt_("dve", a_[:, :], u[:, 1:513], cw[:, l, 1, ch:ch + 1], a_[:, :], ALU.mult, ALU.add, [uk, ak, "cw"], [ak])
                            stt_("dve", a_[:, :], u[:, 0:512], cw[:, l, 0, ch:ch + 1], a_[:, :], ALU.mult, ALU.add, [uk, ak, "cw"], [ak])
                            res.append((a_, ak))
                        (ag, agk), (av, avk) = res
                        act(ag[:, :], ag[:, :], AF.Silu, [agk], [agk])
                        tt_("dve", X[:, fcl * 2 + tt, :], ag[:, :], av[:, :], ALU.mult, [agk, avk], [("X", fcl * 2 + tt)])

            def consume(c128, tt, b):
                cs = slice(tt * 512, tt * 512 + 512)
                tt_("dve", hT[:, c128, cs], hT[:, c128, cs], PS[b][:, :], ALU.add, [("hT", c128, tt), ("ps", b)], [("hT", c128, tt)])

            proj_fm(Wd[fq * 11 * 128:(fq + 1) * 11 * 128, :], 8, 0, lambda kc, tt: X[:, kc * 2 + tt, :],
                    lambda kc, tt: [("X", kc * 2 + tt)], 11, consume)
        P.barrier()
        A.reset(m0)

    def ple_phase(l, j):
        t0 = j * T
        m0 = A.mark()
        pTc = A.alloc("pTc", [128, 2, T], BF16)
        pTf = A.alloc("pTf", [128, 2, T], F32)
        sg = [A.alloc("sg", [128, 512], F32) for _ in range(3)]
        dma("sp", "pTf", pTf[:, :, :], pT[l, :, t0:t0 + T].rearrange("(k p) t -> p k t", p=128), [], ["pTf"])
        for k_ in range(2):
            act(pTc[:, k_, :], pTf[:, k_, :], AF.Copy, ["pTf"], ["pTc"])
        n = 0
        for cbk in range(8):
            c0 = cbk * 256
            gblk, gkey = load_wblock(w_pg[l][:, c0:c0 + 256], KC, 256)
            pblk, pkey = load_wblock(w_pl[l][:, c0:c0 + 256], 2, 256)
            for sub in range(2):
                dc = cbk * 2 + sub
                for tt in range(2):
                    cs = slice(tt * 512, tt * 512 + 512)
                    bg = next_bank()
                    for kc in range(KC):
                        mm(PS[bg][:, :], gblk[:, kc, sub * 128:(sub + 1) * 128], aT_rhs(kc, tt), kc == 0, kc == KC - 1,
                           [gkey, ("aT", kc, tt)], [("ps", bg)])
                    bp = next_bank()
                    for kc in range(2):
                        mm(PS[bp][:, :], pblk[:, kc, sub * 128:(sub + 1) * 128], pTc[:, kc, cs], kc == 0, kc == 1,
                           [pkey, "pTc"], [("ps", bp)])
                    s_ = sg[n % 3]
                    sk = ("sg", n % 3)
                    n += 1
                    act(s_[:, :], PS[bg][:, :], AF.Sigmoid, [("ps", bg)], [sk])
                    tt_("dve", s_[:, :], s_[:, :], PS[bp][:, :], ALU.mult, [sk, ("ps", bp)], [sk])
                    tt_("dve", hT[:, dc, cs], hT[:, dc, cs], s_[:, :], ALU.add, [("hT", dc, tt), sk], [("hT", dc, tt)])
        P.barrier()
        A.reset(m0)

    for j in range(n_chunks):
        t0 = j * T
        for q in range(4):
            dma("sp", ("hTload", q), hT[:, q * 4:(q + 1) * 4, :],
                xT[q * 512:(q + 1) * 512, t0:t0 + T].rearrange("(k p) t -> p k t", p=128), [],
                [("hT", kc, tt) for kc in range(q * 4, q * 4 + 4) for tt in range(2)])
        for l in range(n_layers):
            if do_mix:
                _ph(f"c{j}l{l}:norm")
                if l % 2 == 1:
                    mk_ = rmsnorm_phase(l, hold=True)
                else:
                    mk_ = rmsnorm_phase(l)
                _ph(f"c{j}l{l}:qkv")
                qkv_phase(l, j)
                A.reset(mk_)
                if l % 2 == 0:
                    _ph(f"c{j}l{l}:sb")
                    sb_attention(l, j, list(range(8)))
                    _ph(f"c{j}l{l}:dil")
                    dil_attention(l, j, list(range(8, 16)))
                else:
                    _ph(f"c{j}l{l}:fox")
                    fox_attention(l, j)
                _ph(f"c{j}l{l}:wout")
                wout_phase(l)
            if do_ffn:
                _ph(f"c{j}l{l}:norm")
                mk_ = rmsnorm_phase(4 + l, hold=True)
                _ph(f"c{j}l{l}:ffn")
                ffn_phase(l, j)
                A.reset(mk_)
            if do_ple:
                _ph(f"c{j}l{l}:norm")
                mk_ = rmsnorm_phase(8 + l, hold=True)
                _ph(f"c{j}l{l}:ple")
                ple_phase(l, j)
                A.reset(mk_)
        _ph(f"c{j}:final")
        rmsnorm_phase(12, final_chunk=j)
    _ph("end")
    P.wait_for("sp", out_pids)

    with ExitStack() as es:
        sems = {e: [es.enter_context(nc.semaphore(f"s_{e}_{i}")) for i in range(NROT)] for e in COMPUTE}
        chan_sems = {c: es.enter_context(nc.semaphore(f"c_{c.name}")) for c in P.chans}
        emit_program(nc, P, sems, chan_sems)
    return nc, A.peak


_CACHE = {}


def _layout_inputs(inp, b):
    f32 = np.float32
    cfc, cbc = make_consts()
    gains = np.zeros((128, 13, KC), f32)
    for n, arr in enumerate(list(inp["attn_norm"]) + list(inp["ffn_norm"]) + list(inp["ple_norm"]) + [inp["final_norm"]]):
        gains[:, n, :] = np.asarray(arr, f32).reshape(KC, 128).T
    cw = np.zeros((128, DEPTH, 4, 88), f32)
    for l in range(DEPTH):
        for t in range(3):
            cw[:, l, t, :] = np.asarray(inp["conv_w"][l, t], f32).reshape(88, 128).T
        cw[:, l, 3, :] = np.asarray(inp["conv_b"][l], f32).reshape(88, 128).T
    m = {
        "xT": np.ascontiguousarray(np.asarray(inp["x"][b], f32).T),
        "pT": np.ascontiguousarray(np.transpose(np.asarray(inp["p"][:, b], f32), (0, 2, 1))),
        "pos": np.ascontiguousarray(np.asarray(inp["positions"][b], np.int32).reshape(1, SEQ)),
        "gains": gains.reshape(128, -1),
        "cw": cw.reshape(128, -1),
        "bfg": np.asarray(inp["b_forget"], f32).reshape(1, 32),
        "cf": cfc,
        "cb": cbc,
    }
    for k in ("w_in_even", "w_out_even", "w_in_odd", "w_out_odd", "w_up", "w_down", "w_ple_gate", "w_ple"):
        m[k] = np.ascontiguousarray(np.asarray(inp[k], f32))
    return m


def run(inputs, n_layers=DEPTH, n_chunks=NCHUNK, do_mix=True, do_ffn=True, do_ple=True, trace=False):
    key = (n_layers, n_chunks, do_mix, do_ffn, do_ple)
    if key not in _CACHE:
        _CACHE[key] = build_program(*key)[0]
    nc = _CACHE[key]
    in_maps = [_layout_inputs(inputs, b) for b in range(NB)]
    res = run_bass_kernel_spmd(nc, in_maps, core_ids=list(range(NB)), trace=trace)
    out = np.stack([np.ascontiguousarray(res.results[b]["yT"].T) for b in range(NB)], axis=0)
    return out.astype(np.float32), res


def kernel(**inputs):
    out, _ = run(inputs)
    return out
```

```python
import math
from contextlib import ExitStack

import numpy as np
import concourse.bass as bass
import concourse.mybir as mybir
from concourse.bass_utils import run_bass_kernel_spmd

F32 = mybir.dt.float32
BF16 = mybir.dt.bfloat16
I32 = mybir.dt.int32
ALU = mybir.AluOpType
AF = mybir.ActivationFunctionType

D = 2048
SEQ = 4096
NB = 2
DEPTH = 4
HD = 128
NH = 16
FF = 5632
PLE = 256
T = 1024
NCHUNK = SEQ // T
KC = D // 128
EPS = 1e-6
QSCALE = HD ** -0.5
TWO_PI = 2.0 * math.pi

COMPUTE = ("pe", "act", "dve", "pool")
NROT = 8


class Chan:
    def __init__(self, name):
        self.name = name
        self.n = 0


class Prog:
    def __init__(self):
        self.ins = {e: [] for e in ("pe", "act", "dve", "pool", "sp")}
        self.last_writer = {}
        self.readers = {}
        self.seen = {e: {} for e in self.ins}
        self.chans = []
        self.persist = set()
        self.persist_chans = set()

    def chan(self, name):
        c = Chan(name)
        self.chans.append(c)
        return c

    def _need(self, eng, pid, rec):
        stream, idx = pid
        s = self.seen[eng]
        if s.get(stream, -1) >= idx:
            return
        s[stream] = idx
        rec["waits"].append(pid)
        if isinstance(stream, str):
            self.ins[stream][idx]["marked"] = True

    def op(self, eng, fn, reads=(), writes=(), chan=None):
        lst = self.ins[eng]
        rec = {"fn": fn, "waits": [], "marked": False, "chan": chan}
        is_dma = chan is not None
        if is_dma:
            pid = (chan, chan.n)
            chan.n += 1
        else:
            pid = (eng, len(lst))
        deps = []
        for k in reads:
            w = self.last_writer.get(k)
            if w is not None and not ((not is_dma) and w[0] == eng and eng == "pe"):
                deps.append(w)
        for k in writes:
            w = self.last_writer.get(k)
            if w is not None and not ((not is_dma) and w[0] == eng):
                deps.append(w)
            for r in self.readers.get(k, ()):
                if (not is_dma) and r[0] == eng:
                    continue
                deps.append(r)
        for d in deps:
            self._need(eng, d, rec)
        for k in writes:
            self.last_writer[k] = pid
            self.readers[k] = []
        for k in reads:
            self.readers.setdefault(k, []).append(pid)
        lst.append(rec)
        return pid

    def wait_for(self, eng, pids):
        rec = {"fn": None, "waits": [], "marked": False, "chan": None}
        for p in pids:
            self._need(eng, p, rec)
        self.ins[eng].append(rec)

    def barrier(self):
        pids = []
        for e in COMPUTE:
            for i in range(len(self.ins[e]) - 1, -1, -1):
                if self.ins[e][i]["fn"] is not None and self.ins[e][i]["chan"] is None:
                    pids.append((e, i))
                    break
        for c in self.chans:
            if c.n and c not in self.persist_chans:
                pids.append((c, c.n - 1))
        for e in self.ins:
            if e == "pool":
                continue
            self.wait_for(e, pids)
        self.last_writer = {k: v for k, v in self.last_writer.items() if k in self.persist}
        self.readers = {k: v for k, v in self.readers.items() if k in self.persist}


def emit_program(nc, prog, sems, chan_sems):
    ordinal = {}
    for e in COMPUTE:
        m = 0
        for i, rec in enumerate(prog.ins[e]):
            if rec["marked"]:
                ordinal[(e, i)] = m
                m += 1

    def run(eng_name, eng):
        for i, rec in enumerate(prog.ins[eng_name]):
            for (stream, idx) in rec["waits"]:
                if isinstance(stream, str):
                    m = ordinal[(stream, idx)]
                    eng.wait_ge(sems[stream][m % NROT], m // NROT + 1)
                else:
                    eng.wait_ge(chan_sems[stream], 16 * (idx + 1))
            if rec["fn"] is None:
                continue
            ins = rec["fn"](eng)
            if rec["chan"] is not None:
                ins.then_inc(chan_sems[rec["chan"]], 16)
            elif rec["marked"]:
                m = ordinal[(eng_name, i)]
                ins.then_inc(sems[eng_name][m % NROT], 1)

    with nc.Block() as block:
        @block.tensor
        def _(e):
            run("pe", e)

        @block.scalar
        def _(e):
            run("act", e)

        @block.vector
        def _(e):
            run("dve", e)

        @block.gpsimd
        def _(e):
            run("pool", e)

        @block.sync
        def _(e):
            run("sp", e)


class Arena:
    def __init__(self, nc, base=16512, limit=229376):
        self.nc = nc
        self.off = base
        self.limit = limit
        self.n = 0
        self.peak = base

    def alloc(self, name, shape, dtype):
        isz = 4 if dtype in (F32, I32) else 2
        size = isz
        for s in shape[1:]:
            size *= s
        size = (size + 63) // 64 * 64
        if self.off + size > self.limit:
            raise RuntimeError(f"SBUF arena overflow allocating {name} {shape}: off={self.off} size={size}")
        self.n += 1
        t = self.nc.alloc_sbuf_tensor_at(f"{name}_{self.n}", list(shape), dtype, offset=self.off)
        self.off += size
        self.peak = max(self.peak, self.off)
        return t

    def mark(self):
        return self.off

    def reset(self, m):
        self.off = m


C_IDENT, C_ONES, C_UINCL, C_PM, C_INVF, C_SHC, C_SHS, C_NF = 0, 128, 256, 384, 416, 417, 418, 419
B_ONES, B_MASKF, B_MASKS, B_MASKA, B_USTR, B_LINC, B_NUGE, B_NONE, B_NB = 0, 128, 256, 384, 512, 640, 768, 896, 1024


def make_consts():
    cf = np.zeros((128, C_NF), np.float32)
    idx = np.arange(128)
    cf[:, C_IDENT:C_IDENT + 128] = np.eye(128, dtype=np.float32)
    cf[:, C_ONES:C_ONES + 128] = 1.0
    cf[:, C_UINCL:C_UINCL + 128] = (idx[:, None] <= idx[None, :]).astype(np.float32)
    pm = np.zeros((32, 32), np.float32)
    for m in range(32):
        pm[(m + 16) % 32, m] = 1.0
    cf[:32, C_PM:C_PM + 32] = pm
    inv_freq = (np.float32(500000.0) ** (-np.arange(0, 32, 2, dtype=np.float32) / np.float32(32.0))).astype(np.float32)
    cf[:32, C_INVF] = np.concatenate([inv_freq, inv_freq])
    cf[:32, C_SHC] = 0.5 * math.pi
    cf[:16, C_SHS] = math.pi
    cf[16:32, C_SHS] = 0.0
    cb = np.zeros((128, B_NB), np.float32)
    cb[:, B_ONES:B_ONES + 128] = 1.0
    cb[:, B_MASKF:B_MASKF + 128] = (idx[:, None] <= idx[None, :])
    cb[:, B_MASKS:B_MASKS + 128] = (idx[:, None] < idx[None, :])
    cb[:, B_MASKA:B_MASKA + 128] = (idx[:, None] >= idx[None, :])
    cb[:, B_USTR:B_USTR + 128] = (idx[:, None] > idx[None, :])
    cb[:, B_LINC:B_LINC + 128] = (idx[:, None] <= idx[None, :])
    cb[:, B_NUGE:B_NUGE + 128] = -1.0 * (idx[:, None] >= idx[None, :])
    cb[:, B_NONE:B_NONE + 128] = -1.0
    return cf, cb


MMCOUNT = [0]
PHASES = []


def _ph(name):
    PHASES.append((name, MMCOUNT[0]))


def build_program(n_layers=DEPTH, n_chunks=NCHUNK, do_mix=True, do_ffn=True, do_ple=True):
    nc = bass.Bass("TRN2", target_bir_lowering=False)
    P = Prog()
    MMCOUNT[0] = 0
    del PHASES[:]

    def din(name, shape, dt=F32):
        return nc.dram_tensor(name, list(shape), dt, kind="ExternalInput").ap()

    xT = din("xT", [D, SEQ])
    pT = din("pT", [DEPTH, PLE, SEQ])
    pos = din("pos", [1, SEQ], I32)
    gains_d = din("gains", [128, 13 * KC])
    cw_d = din("cw", [128, DEPTH * 4 * 88])
    bf_d = din("bfg", [1, 32])
    cf_d = din("cf", [128, C_NF])
    cb_d = din("cb", [128, B_NB])
    w_in_even = din("w_in_even", [2, D, 3 * D])
    w_out_even = din("w_out_even", [2, D, D])
    w_in_odd = din("w_in_odd", [2, D, 3 * D + NH])
    w_out_odd = din("w_out_odd", [2, D, D])
    w_up = din("w_up", [DEPTH, D, 2 * FF])
    w_down = din("w_down", [DEPTH, FF, D])
    w_pg = din("w_ple_gate", [DEPTH, D, D])
    w_pl = din("w_ple", [DEPTH, PLE, D])
    yT = nc.dram_tensor("yT", [D, SEQ], F32, kind="ExternalOutput").ap()
    kT_s = nc.dram_tensor("kT_s", [DEPTH, NH, 128, SEQ], BF16).ap()
    v_s = nc.dram_tensor("v_s", [DEPTH, NH, SEQ, 128], BF16).ap()

    A = Arena(nc)
    hT = A.alloc("hT", [128, KC, T], F32)
    aT = A.alloc("aT", [128, KC, T], BF16)
    X = A.alloc("X", [128, 32, 512], BF16)
    NWB = 4
    WB = [A.alloc("wb", [128, 16, 256], BF16) for _ in range(NWB)]
    negc = A.alloc("negc", [128, 2, 32, NH], F32)
    ctot = A.alloc("ctot", [128, 2, NH], F32)
    ucar = A.alloc("ucar", [128, DEPTH, 88, 2], F32)
    gains = A.alloc("gains", [128, 13, KC], F32)
    cw = A.alloc("cw", [128, DEPTH, 4, 88], F32)
    cf = A.alloc("cf", [128, C_NF], F32)
    cb = A.alloc("cb", [128, B_NB], BF16)
    bfb = A.alloc("bfb", [128, 32], F32)
    persist_mark = A.mark()

    for i_ in range(NWB):
        P.persist.add(("wb", i_))
    PS = [nc.alloc_psum_tensor(f"ps{i}", [128, 512], F32) for i in range(8)]

    ident = cf[:, C_IDENT:C_IDENT + 128]
    ones_f = cf[:, C_ONES:C_ONES + 128]
    uincl_f = cf[:, C_UINCL:C_UINCL + 128]
    ones_b = cb[:, B_ONES:B_ONES + 128]

    chans = {}

    def chan_of(ckey):
        if ckey not in chans:
            chans[ckey] = P.chan("c%d" % len(chans))
        return chans[ckey]

    def dma(eng, ckey, out, in_, reads, writes):
        return P.op(eng, lambda e: e.dma_start(out=out, in_=in_), reads, writes, chan=chan_of(ckey))

    def mm(out, lhsT, rhs, start, stop, reads, writes):
        MMCOUNT[0] += 1
        return P.op("pe", lambda e: e.matmul(out, lhsT=lhsT, rhs=rhs, start=start, stop=stop, skip_group_check=True), reads, writes)

    def act(out, in_, func, reads, writes, bias=None, scale=None):
        kw = {}
        if bias is not None:
            kw["bias"] = bias
        if scale is not None:
            kw["scale"] = scale
        return P.op("act", lambda e: e.activation(out=out, in_=in_, func=func, **kw), reads, writes)

    def tt_(eng, out, in0, in1, op, reads, writes):
        return P.op(eng, lambda e: e.tensor_tensor(out=out, in0=in0, in1=in1, op=op), reads, writes)

    def ts_(eng, out, in0, s1, s2, op0, op1, reads, writes):
        return P.op(eng, lambda e: e.tensor_scalar(out=out, in0=in0, scalar1=s1, scalar2=s2, op0=op0, op1=op1), reads, writes)

    def stt_(eng, out, in0, scalar, in1, op0, op1, reads, writes):
        return P.op(eng, lambda e: e.scalar_tensor_tensor(out=out, in0=in0, scalar=scalar, in1=in1, op0=op0, op1=op1), reads, writes)

    def cp_(eng, out, in_, reads, writes):
        return P.op(eng, lambda e: e.tensor_copy(out=out, in_=in_), reads, writes)

    def memset_(eng, ap, val, writes):
        return P.op(eng, lambda e: e.memset(ap, val), (), writes)

    dma("sp", "gains", gains[:, :, :], gains_d.rearrange("p (n k) -> p n k", k=KC), [], ["gains"])
    dma("sp", "cw", cw[:, :, :, :], cw_d.rearrange("p (l t f) -> p l t f", l=DEPTH, t=4), [], ["cw"])
    dma("sp", "cf", cf[:, :], cf_d[:, :], [], ["cf"])
    dma("pool", "cb", cb[:, :], cb_d[:, :], [], ["cb"])
    dma("sp", "bfb", bfb[:, :], bf_d[0:1, :].broadcast_to([128, 32]), [], ["bfb"])
    memset_("dve", ctot[:, :, :], 0.0, ["ctot"])
    memset_("dve", ucar[:, :, :, :], 0.0, ["ucar"])

    wcount = [0]

    def load_wblock(src_ap, nk, ncols):
        i = wcount[0] % NWB
        wcount[0] += 1
        blk = WB[i]
        dma("pool", ("wb", i), blk[:, 0:nk, 0:ncols], src_ap.rearrange("(k p) c -> p k c", p=128), [], [("wb", i)])
        P.persist_chans.add(chan_of(("wb", i)))
        return blk, ("wb", i)

    bank_rr = [0]

    def next_bank():
        b = bank_rr[0] % 8
        bank_rr[0] += 1
        return b

    def rmsnorm_phase(gi, final_chunk=None, hold=False):
        m0 = A.mark()
        sq = [A.alloc("sq", [128, 512], BF16) for _ in range(3)]
        rstd = [A.alloc("rstd", [128, 512], F32) for _ in range(2)]
        yst = [A.alloc("yst", [128, 512], F32) for _ in range(3)] if final_chunk is not None else None
        for tt in range(2):
            b = next_bank()
            cs = slice(tt * 512, tt * 512 + 512)
            for kc in range(KC):
                s = sq[kc % 3]
                act(s[:, :], hT[:, kc, cs], AF.Square, [("hT", kc, tt)], [("sq", kc % 3)])
                mm(PS[b][:, :], ones_b, s[:, :], kc == 0, kc == KC - 1, [("sq", kc % 3), "cb"], [("ps", b)])
            r = rstd[tt]
            act(r[:, :], PS[b][:, :], AF.Ln, [("ps", b)], [("rstd", tt)], bias=EPS, scale=1.0 / D)
            act(r[:, :], r[:, :], AF.Exp, [("rstd", tt)], [("rstd", tt)], scale=-0.5)
            for kc in range(KC):
                if final_chunk is None:
                    stt_("dve", aT[:, kc, cs], hT[:, kc, cs], gains[:, gi, kc:kc + 1], r[:, :], ALU.mult, ALU.mult,
                         [("hT", kc, tt), ("rstd", tt), "gains"], [("aT", kc, tt)])
                else:
                    y = yst[kc % 3]
                    stt_("dve", y[:, :], hT[:, kc, cs], gains[:, gi, kc:kc + 1], r[:, :], ALU.mult, ALU.mult,
                         [("hT", kc, tt), ("rstd", tt), "gains"], [("yst", kc % 3)])
                    t0 = final_chunk * T + tt * 512
                    pid = dma("sp", ("yst", kc % 3), yT[kc * 128:(kc + 1) * 128, t0:t0 + 512], y[:, :], [("yst", kc % 3)], [])
                    out_pids.append(pid)
        if hold:
            return m0
        P.barrier()
        A.reset(m0)
        return m0

    out_pids = []

    def proj_fm(W2d, ncolblocks, col0, rhs_fn, rhs_keys_fn, nk_total, consume, tts=(0, 1)):
        kgroups = []
        k = 0
        while k < nk_total:
            kgroups.append((k, min(16, nk_total - k)))
            k += 16
        for cbk in range(ncolblocks):
            c0 = col0 + cbk * 256
            banks = {}
            for gi_, (k0, nk) in enumerate(kgroups):
                blk, wkey = load_wblock(W2d[k0 * 128:(k0 + nk) * 128, c0:c0 + 256], nk, 256)
                for sub in range(2):
                    for tt in tts:
                        if gi_ == 0:
                            banks[(sub, tt)] = next_bank()
                        b = banks[(sub, tt)]
                        for kc in range(nk):
                            gk = k0 + kc
                            mm(PS[b][:, :], blk[:, kc, sub * 128:(sub + 1) * 128], rhs_fn(gk, tt),
                               gk == 0, gk == nk_total - 1, [wkey] + rhs_keys_fn(gk, tt), [("ps", b)])
            for sub in range(2):
                for tt in tts:
                    consume((c0 - col0) // 128 + sub, tt, banks[(sub, tt)])

    def aT_rhs(kc, tt):
        return aT[:, kc, tt * 512:(tt + 1) * 512]

    def aT_keys(kc, tt):
        return [("aT", kc, tt)]

    def qo(h, tt):
        return X[:, 2 * h + tt, :]

    def qo_key(h, tt):
        return ("X", 2 * h + tt)

    def qkv_phase(l, j):
        even = (l % 2 == 0)
        W = (w_in_even if even else w_in_odd)[l // 2]
        t0 = j * T
        m0 = A.mark()
        kst = [A.alloc("kst", [128, 512], BF16) for _ in range(2)]
        vst = [A.alloc("vst", [128, 256], BF16) for _ in range(2)]
        cnt = {"k": 0, "v": 0, "r": 0}
        if even:
            rst = [A.alloc("rst", [128, 512], F32) for _ in range(2)]
            rt1 = [A.alloc("rt1", [32, 512], F32) for _ in range(2)]
            rt2 = [A.alloc("rt2", [32, 512], F32) for _ in range(2)]
            ropeC = A.alloc("ropeC", [32, T], F32)
            ropeS = A.alloc("ropeS", [32, T], F32)
            posi = A.alloc("posi", [32, 512], I32)
            rtmp = A.alloc("rtmp", [32, 512], F32)
            MAGIC = 12582912.0
            PI_C = 3.14159
            for hf in range(2):
                hs = slice(hf * 512, hf * 512 + 512)
                dma("sp", "posi", posi[:, :], pos[0:1, t0 + hf * 512:t0 + hf * 512 + 512].broadcast_to([32, 512]), [], ["posi"])
                cp_("dve", ropeC[:, hs], posi[:, :], ["posi"], [("ropeC", hf)])
                for tab, shc, nm0 in ((ropeS, C_SHS, "ropeS"), (ropeC, C_SHC, "ropeC")):
                    nm = (nm0, hf)
                    ts_("dve", tab[:, hs], ropeC[:, hs], cf[0:32, C_INVF:C_INVF + 1], cf[0:32, shc:shc + 1], ALU.mult, ALU.add,
                        [("ropeC", hf), "cf"], [nm])
                    ts_("dve", rtmp[:, :], tab[:, hs], 1.0 / TWO_PI, MAGIC, ALU.mult, ALU.add, [nm], ["rtmp"])
                    ts_("dve", rtmp[:, :], rtmp[:, :], -MAGIC, 0.0, ALU.add, ALU.add, ["rtmp"], ["rtmp"])
                    stt_("dve", tab[:, hs], rtmp[:, :], -TWO_PI, tab[:, hs], ALU.mult, ALU.add, ["rtmp", nm], [nm])
                    ts_("dve", tab[:, hs], tab[:, hs], -PI_C, PI_C, ALU.max, ALU.min, [nm], [nm])
                    act(tab[:, hs], tab[:, hs], AF.Sin, [nm], [nm])

        def rope_evac(b, tt, dst_ap, dst_key, scale):
            i = cnt["r"] % 2
            cnt["r"] += 1
            cs = slice(tt * 512, tt * 512 + 512)
            r = rst[i]
            act(r[:, :], PS[b][:, :], AF.Copy, [("ps", b)], [("rst", i)], scale=scale)
            b2 = next_bank()
            mm(PS[b2][0:32, :], cf[0:32, C_PM:C_PM + 32], r[0:32, :], True, True, [("rst", i), "cf"], [("ps", b2)])
            cp_("dve", dst_ap, r[:, :], [("rst", i)], [dst_key])
            tt_("dve", rt1[i][:, :], r[0:32, :], ropeC[:, cs], ALU.mult, [("rst", i), ("ropeC", tt)], [("rt1", i)])
            tt_("dve", rt2[i][:, :], PS[b2][0:32, :], ropeS[:, cs], ALU.mult, [("ps", b2), ("ropeS", tt)], [("rt2", i)])
            tt_("dve", dst_ap[0:32, :], rt1[i][:, :], rt2[i][:, :], ALU.add, [("rt1", i), ("rt2", i)], [dst_key])

        def consume_q(c128, tt, b):
            h = c128
            if even and h >= 8:
                rope_evac(b, tt, qo(h, tt), qo_key(h, tt), QSCALE)
            else:
                act(qo(h, tt), PS[b][:, :], AF.Copy, [("ps", b)], [qo_key(h, tt)], scale=QSCALE)

        def consume_k(c128, tt, b):
            h = c128
            i = cnt["k"] % 2
            cnt["k"] += 1
            ks = kst[i]
            if even and h >= 8:
                rope_evac(b, tt, ks[:, :], ("kst", i), 1.0)
            else:
                cp_("dve", ks[:, :], PS[b][:, :], [("ps", b)], [("kst", i)])
            dma("sp", ("kst", i), kT_s[l, h, :, t0 + tt * 512:t0 + tt * 512 + 512], ks[:, :], [("kst", i)], [("kTs", l, h, j, tt)])

        proj_fm(W, 8, 0, aT_rhs, aT_keys, KC, consume_q)
        proj_fm(W, 8, D, aT_rhs, aT_keys, KC, consume_k)
        for cbk in range(8):
            c0 = 2 * D + cbk * 256
            blk, wkey = load_wblock(W[:, c0:c0 + 256], KC, 256)
            for mt in range(8):
                b = next_bank()
                for kc in range(KC):
                    mm(PS[b][:, 0:256], aT[:, kc, mt * 128:(mt + 1) * 128], blk[:, kc, :], kc == 0, kc == KC - 1,
                       [wkey, ("aT", kc, mt // 4)], [("ps", b)])
                i = cnt["v"] % 2
                cnt["v"] += 1
                vs = vst[i]
                cp_("dve", vs[:, :], PS[b][:, 0:256], [("ps", b)], [("vst", i)])
                for sub in range(2):
                    h = 2 * cbk + sub
                    dma("sp", ("vst", i, sub), v_s[l, h, t0 + mt * 128:t0 + (mt + 1) * 128, :], vs[:, sub * 128:(sub + 1) * 128],
                        [("vst", i)], [("vs", l, h, j, mt)])
        if not even:
            fl = l // 2
            nlf = [A.alloc("nlf", [128, NH], F32) for _ in range(2)]
            blk, wkey = load_wblock(W[:, 3 * D:3 * D + NH], KC, NH)
            for mt in range(8):
                g = j * 8 + mt
                b = next_bank()
                for kc in range(KC):
                    mm(PS[b][:, 0:NH], aT[:, kc, mt * 128:(mt + 1) * 128], blk[:, kc, 0:NH], kc == 0, kc == KC - 1,
                       [wkey, ("aT", kc, mt // 4)], [("ps", b)])
                n = nlf[mt % 2]
                nk_ = ("nlf", mt % 2)
                tt_("dve", n[:, :], PS[b][:, 0:NH], bfb[:, fl * NH:(fl + 1) * NH], ALU.add, [("ps", b), "bfb"], [nk_])
                act(n[:, :], n[:, :], AF.Exp, [nk_], [nk_], scale=-1.0)
                act(n[:, :], n[:, :], AF.Ln, [nk_], [nk_], bias=1.0)
                b2 = next_bank()
                mm(PS[b2][:, 0:NH], uincl_f, n[:, :], True, True, [nk_, "cf"], [("ps", b2)])
                mm(PS[b2][:, NH:2 * NH], ones_f, n[:, :], True, True, [nk_, "cf"], [("ps", b2)])
                tt_("dve", negc[:, fl, g, :], PS[b2][:, 0:NH], ctot[:, fl, :], ALU.add, [("ps", b2), "ctot"], [("negc", fl, g)])
                tt_("dve", ctot[:, fl, :], PS[b2][:, NH:2 * NH], ctot[:, fl, :], ALU.add, [("ps", b2), "ctot"], ["ctot"])
        P.barrier()
        A.reset(m0)

    def pipeline(steps, offsets):
        n = len(steps)
        ns = len(offsets)
        for t_ in range(n + offsets[-1]):
            for st in range(ns - 1, -1, -1):
                i_ = t_ - offsets[st]
                if 0 <= i_ < n:
                    steps[i_][st]()

    def load_kv(l, h, c, kbuf, vbuf, i):
        dma("sp", ("kbuf", i), kbuf[i][:, :], kT_s[l, h, :, c * T:(c + 1) * T], [("kTs", l, h, c, tt) for tt in range(2)], [("kbuf", i)])
        dma("sp", ("vbuf", i), vbuf[i][:, :, :], v_s[l, h, c * T:(c + 1) * T, :].rearrange("(m p) d -> p m d", p=128),
            [("vs", l, h, c, mt) for mt in range(8)], [("vbuf", i)])

    def fox_attention(l, j):
        fl = l // 2
        m0 = A.mark()
        kbuf = [A.alloc("kbuf", [128, T], BF16) for _ in range(2)]
        vbuf = [A.alloc("vbuf", [128, 8, 128], BF16) for _ in range(2)]
        NBF = 4
        PT = [A.alloc("PT", [128, 512], BF16) for _ in range(NBF)]
        tmp = [A.alloc("tmpF", [128, 512], F32) for _ in range(3)]
        cqb = [A.alloc("cqb", [128, T], F32) for _ in range(2)]
        dg = [A.alloc("dg", [128, 128], F32) for _ in range(2)]
        ZB = [0, 1, 2]
        OB = [3, 4]
        RB = [5, 6]
        CQ = 7
        maskf = cb[:, B_MASKF:B_MASKF + 128]
        cnt = {"load": 0, "step": 0}

        def prep_cqb(h):
            cq = cqb[h % 2]
            for half in range(2):
                for q4 in range(4):
                    qb = half * 4 + q4
                    g = j * 8 + qb
                    d_ = dg[qb % 2]
                    ts_("dve", d_[:, :], ident, negc[:, fl, g, h:h + 1], -1.0, ALU.mult, ALU.mult,
                        ["cf", ("negc", fl, g)], [("dg", qb % 2)])
                    mm(PS[CQ][:, q4 * 128:(q4 + 1) * 128], ones_f, d_[:, :], True, True, [("dg", qb % 2), "cf"], [("ps", CQ)])
                act(cq[:, half * 512:(half + 1) * 512], PS[CQ][:, :], AF.Copy, [("ps", CQ)], [("cqb", h % 2, half)])

        prep_cqb(0)
        segs = [(h_, c_) for h_ in range(NH) for c_ in range(j + 1)]
        load_kv(l, segs[0][0], segs[0][1], kbuf, vbuf, 0)
        for h in range(NH):
            cq = cqb[h % 2]
            steps = []
            for c in range(j + 1):
                i = cnt["load"] % 2
                nxt = cnt["load"] + 1
                cnt["load"] += 1
                ld = [0]
                for qt in range(2):
                    for ktl in range(8):
                        rel = None
                        if c == j:
                            if ktl > 4 * qt + 3:
                                continue
                            if ktl >= 4 * qt:
                                rel = ktl - 4 * qt
                        off = 0 if rel is None else rel * 128
                        first = (c == 0 and ktl == 0)
                        last = (c == j and ktl == 4 * qt + 3)
                        g = c * 8 + ktl
                        sk = cnt["step"] % NBF
                        zb = ZB[cnt["step"] % 3]
                        cnt["step"] += 1
                        ncol = 512 - off
                        do_load = (ld[0] == 4)
                        ld[0] += 1

                        def s1(i=i, c=c, ktl=ktl, qt=qt, off=off, ncol=ncol, zb=zb, do_load=do_load, h=h, nxt=nxt):
                            if do_load and nxt < len(segs):
                                load_kv(l, segs[nxt][0], segs[nxt][1], kbuf, vbuf, nxt % 2)
                            mm(PS[zb][:, 0:ncol], kbuf[i][:, ktl * 128:(ktl + 1) * 128], qo(h, qt)[:, off:512], True, True,
                               [("kbuf", i), qo_key(h, qt)], [("ps", zb)])

                        def s2a(qt=qt, off=off, ncol=ncol, zb=zb, sk=sk, g=g, h=h, cq=cq):
                            stt_("dve", tmp[sk % 3][:, 0:ncol], PS[zb][:, 0:ncol], negc[:, fl, g, h:h + 1], cq[:, qt * 512 + off:(qt + 1) * 512],
                                 ALU.add, ALU.add, [("ps", zb), ("negc", fl, g), ("cqb", h % 2, qt)], [("tmp", sk % 3)])

                        def s2(ncol=ncol, sk=sk):
                            act(PT[sk][:, 0:ncol], tmp[sk % 3][:, 0:ncol], AF.Exp, [("tmp", sk % 3)], [("PT", sk)])

                        def s2m(sk=sk, rel=rel):
                            if rel is not None:
                                tt_("dve", PT[sk][:, 0:128], PT[sk][:, 0:128], maskf, ALU.mult, [("PT", sk), "cb"], [("PT", sk)])

                        def s3(i=i, ktl=ktl, qt=qt, off=off, ncol=ncol, sk=sk, first=first, last=last):
                            mm(PS[OB[qt]][:, off:512], vbuf[i][:, ktl, :], PT[sk][:, 0:ncol], first, last, [("vbuf", i), ("PT", sk)], [("ps", OB[qt])])
                            mm(PS[RB[qt]][:, off:512], ones_b, PT[sk][:, 0:ncol], first, last, [("PT", sk), "cb"], [("ps", RB[qt])])

                        steps.append((s1, s2a, s2, s2m, s3))
            nst = len(steps)
            if h + 1 < NH:
                mid = min(6, nst - 1)
                s1_old = steps[mid][0]
                steps[mid] = ((lambda f=s1_old, hh=h + 1: (prep_cqb(hh), f())),) + tuple(steps[mid][1:])
            pipeline(steps, [0, 1, 2, 3, 5])
            for qt in range(2):
                act(tmp[qt][:, :], PS[RB[qt]][:, :], AF.Ln, [("ps", RB[qt])], [("tmp", qt)])
                act(tmp[qt][:, :], tmp[qt][:, :], AF.Exp, [("tmp", qt)], [("tmp", qt)], scale=-1.0)
                tt_("dve", qo(h, qt), PS[OB[qt]][:, :], tmp[qt][:, :], ALU.mult, [("ps", OB[qt]), ("tmp", qt)], [qo_key(h, qt)])
        P.barrier()
        A.reset(m0)

    def sb_attention(l, j, heads):
        m0 = A.mark()
        kbuf = [A.alloc("kbuf", [128, T], BF16) for _ in range(2)]
        vbuf = [A.alloc("vbuf", [128, 8, 128], BF16) for _ in range(2)]
        NS = 3
        PT = [A.alloc("PT", [128, 512], BF16) for _ in range(NS)]
        LB = [A.alloc("LB", [128, 512], BF16) for _ in range(NS)]
        tES = [A.alloc("tES", [128, 512], F32) for _ in range(2)]
        tT = [A.alloc("tT", [128, 512], F32) for _ in range(2)]
        Cc = [A.alloc("Cc", [128, 512], F32) for _ in range(2)]
        ZB = [0, 1, 2, 3]
        BB = [4, 5]
        OB = [6, 7]
        nuge = cb[:, B_NUGE:B_NUGE + 128]
        nones = cb[:, B_NONE:B_NONE + 128]
        masks = cb[:, B_MASKS:B_MASKS + 128]
        cnt = {"load": 0, "step": 0}
        segs = [(h_, c_) for h_ in heads for c_ in range(j, -1, -1)]
        load_kv(l, segs[0][0], segs[0][1], kbuf, vbuf, 0)
        for h in heads:
            steps = []
            nsteps = {0: 0, 1: 0}
            total = {qt: sum(1 for c in range(j + 1) for ktl in range(8) if not (c == j and ktl > 4 * qt + 3)) for qt in range(2)}
            for qt in range(2):
                memset_("dve", Cc[qt][:, :], 0.0, [("Cc", qt)])
            for c in range(j, -1, -1):
                i = cnt["load"] % 2
                nxt = cnt["load"] + 1
                cnt["load"] += 1
                ld = [0]
                for ktl in range(7, -1, -1):
                    for qt in range(2):
                        rel = None
                        if c == j:
                            if ktl > 4 * qt + 3:
                                continue
                            if ktl >= 4 * qt:
                                rel = ktl - 4 * qt
                        off = 0 if rel is None else rel * 128
                        ncol = 512 - off
                        first = nsteps[qt] == 0
                        nsteps[qt] += 1
                        last = nsteps[qt] == total[qt]
                        n_ = cnt["step"]
                        cnt["step"] += 1
                        zb = ZB[n_ % 4]
                        bb = BB[n_ % 2]
                        s3i = n_ % NS
                        s2i = n_ % 2
                        do_load = (ld[0] == 6)
                        ld[0] += 1

                        def s1(i=i, c=c, ktl=ktl, qt=qt, off=off, ncol=ncol, zb=zb, do_load=do_load, h=h, nxt=nxt):
                            if do_load and nxt < len(segs):
                                load_kv(l, segs[nxt][0], segs[nxt][1], kbuf, vbuf, nxt % 2)
                            mm(PS[zb][:, 0:ncol], kbuf[i][:, ktl * 128:(ktl + 1) * 128], qo(h, qt)[:, off:512], True, False,
                               [("kbuf", i), qo_key(h, qt)], [("ps", zb)])

                        def s2(ncol=ncol, zb=zb, s2i=s2i, s3i=s3i):
                            act(tES[s2i][:, 0:ncol], PS[zb][:, 0:ncol], AF.Exp, [("ps", zb)], [("tES", s2i)])
                            act(LB[s3i][:, 0:ncol], tES[s2i][:, 0:ncol], AF.Ln, [("tES", s2i)], [("LB", s3i)], bias=1.0)

                        def s2b(s3i=s3i, rel=rel):
                            if rel is not None:
                                tt_("dve", LB[s3i][:, 0:128], LB[s3i][:, 0:128], masks, ALU.mult, [("LB", s3i), "cb"], [("LB", s3i)])

                        def s3(ncol=ncol, zb=zb, bb=bb, s3i=s3i):
                            mm(PS[zb][:, 0:ncol], nuge, LB[s3i][:, 0:ncol], False, True, [("LB", s3i), "cb"], [("ps", zb)])
                            mm(PS[bb][:, 0:ncol], nones, LB[s3i][:, 0:ncol], True, True, [("LB", s3i), "cb"], [("ps", bb)])

                        def s4(qt=qt, off=off, ncol=ncol, zb=zb, bb=bb, s2i=s2i):
                            tt_("dve", tT[s2i][:, 0:ncol], PS[zb][:, 0:ncol], Cc[qt][:, off:512], ALU.add, [("ps", zb), ("Cc", qt)], [("tT", s2i)])
                            tt_("dve", Cc[qt][:, off:512], Cc[qt][:, off:512], PS[bb][:, 0:ncol], ALU.add, [("ps", bb), ("Cc", qt)], [("Cc", qt)])

                        def s4b(ncol=ncol, s2i=s2i, s3i=s3i, rel=rel):
                            act(PT[s3i][:, 0:ncol], tT[s2i][:, 0:ncol], AF.Exp, [("tT", s2i)], [("PT", s3i)])
                            if rel is not None:
                                tt_("dve", PT[s3i][:, 0:128], PT[s3i][:, 0:128], masks, ALU.mult, [("PT", s3i), "cb"], [("PT", s3i)])

                        def s5(i=i, ktl=ktl, qt=qt, off=off, ncol=ncol, s3i=s3i, first=first, last=last):
                            mm(PS[OB[qt]][:, off:512], vbuf[i][:, ktl, :], PT[s3i][:, 0:ncol], first, last, [("vbuf", i), ("PT", s3i)], [("ps", OB[qt])])

                        steps.append((s1, s2, s2b, s3, s4, s4b, s5))
            pipeline(steps, [0, 1, 2, 3, 4, 5, 6])
            for qt in range(2):
                act(qo(h, qt), PS[OB[qt]][:, :], AF.Copy, [("ps", OB[qt])], [qo_key(h, qt)])
        P.barrier()
        A.reset(m0)

    def dil_attention(l, j, heads):
        m0 = A.mark()
        t0 = j * T
        lo_chunk = max(0, j - 2)
        nwin = (j - lo_chunk + 1) * T
        wbase = lo_chunk * T
        kwin = [A.alloc("kwin", [128, 3 * T], BF16) for _ in range(2)]
        NV = 6
        NP = 4
        vt = [A.alloc("vt", [128, 128], BF16) for _ in range(NV)]
        PT = [A.alloc("PTd", [128, 128], BF16) for _ in range(NP)]
        rinv = [A.alloc("rinv", [128, 512], F32) for _ in range(2)]
        SB_ = [0, 1, 2]
        OB = [3, 4]
        RB = [5, 6]
        maskf = cb[:, B_MASKF:B_MASKF + 128]
        maska = cb[:, B_MASKA:B_MASKA + 128]
        cnt = {"step": 0, "nv": 0}

        def load_kwin(hh, wi):
            dma("sp", ("kwin", wi), kwin[wi][:, 0:nwin], kT_s[l, hh, :, wbase:wbase + nwin],
                [("kTs", l, hh, c, tt) for c in range(lo_chunk, j + 1) for tt in range(2)], [("kwin", wi)])

        load_kwin(heads[0], 0)
        for hi, h in enumerate(heads):
            wi = hi % 2
            if hi + 1 < len(heads):
                load_kwin(heads[hi + 1], (hi + 1) % 2)
            started = {0: False, 1: False}
            steps = []
            for (dil, nq) in ((1, 128), (4, 128), (16, 64)):
                L = T // dil
                for r in range(dil):
                    for mb in range(L // nq):
                        m_loc = mb * nq
                        M0 = (t0 // dil) + m_loc
                        tiles = []
                        nka = min(128, M0)
                        if nka > 0:
                            tiles.append((M0 - nka, nka, "A" if nka == 128 else None))
                        tiles.append((M0, nq, "F"))
                        for (K0, nk, mk) in tiles:
                            ktok0 = r + dil * K0
                            qtok0 = r + dil * m_loc
                            kcol0 = ktok0 - wbase
                            n_ = cnt["step"]
                            cnt["step"] += 1
                            sb = SB_[n_ % 3]
                            sk = n_ % NP
                            vi = n_ % NV
                            hq, tq = divmod(qtok0, 512)
                            if dil == 16:
                                pieces = [(0, r, 0, 32), (1, r, 32, 64)]
                            else:
                                pieces = [(hq, tq, 0, nq)]
                            flags = []
                            for (tt, _, _, _) in pieces:
                                flags.append(not started[tt])
                                started[tt] = True

                            def s1(dil=dil, r=r, nq=nq, nk=nk, ktok0=ktok0, kcol0=kcol0, sb=sb, vi=vi, hq=hq, tq=tq, h=h, wi=wi):
                                vsrc = v_s[l, h, ktok0:ktok0 + dil * (nk - 1) + 1:dil, :]
                                kc_set = sorted(set([(ktok0) // T, (ktok0 + dil * (nk - 1)) // T]))
                                dma("sp", ("vt", vi), vt[vi][0:nk, :], vsrc, [("vs", l, h, c, mt) for c in kc_set for mt in range(8)], [("vt", vi)])
                                kl = kwin[wi][:, kcol0:kcol0 + dil * (nk - 1) + 1:dil]
                                if dil == 16:
                                    for half in range(2):
                                        qs = qo(h, half)[:, r:r + 16 * 31 + 1:16]
                                        mm(PS[sb][0:nk, half * 32:(half + 1) * 32], kl, qs, True, True, [("kwin", wi), qo_key(h, half)], [("ps", sb)])
                                else:
                                    qs = qo(h, hq)[:, tq:tq + dil * (nq - 1) + 1:dil]
                                    mm(PS[sb][0:nk, 0:nq], kl, qs, True, True, [("kwin", wi), qo_key(h, hq)], [("ps", sb)])

                            def s2a(nq=nq, nk=nk, sb=sb, sk=sk):
                                act(PT[sk][0:nk, 0:nq], PS[sb][0:nk, 0:nq], AF.Exp, [("ps", sb)], [("PTd", sk)])

                            def s2(nq=nq, nk=nk, sk=sk, mk=mk):
                                pt = PT[sk]
                                if mk == "F":
                                    tt_("dve", pt[0:nk, 0:nq], pt[0:nk, 0:nq], maskf[0:nk, 0:nq], ALU.mult, [("PTd", sk), "cb"], [("PTd", sk)])
                                elif mk == "A":
                                    tt_("dve", pt[0:nk, 0:nq], pt[0:nk, 0:nq], maska[0:nk, 0:nq], ALU.mult, [("PTd", sk), "cb"], [("PTd", sk)])

                            def s3(dil=dil, nk=nk, sk=sk, vi=vi, pieces=pieces, flags=flags):
                                pt = PT[sk]
                                for (tt, tq_, a_, b_), first in zip(pieces, flags):
                                    nn = b_ - a_
                                    osl = slice(tq_, tq_ + dil * (nn - 1) + 1, dil)
                                    mm(PS[OB[tt]][:, osl], vt[vi][0:nk, :], pt[0:nk, a_:b_], first, False, [("vt", vi), ("PTd", sk)], [("ps", OB[tt])])
                                    mm(PS[RB[tt]][:, osl], ones_b[0:nk, :], pt[0:nk, a_:b_], first, False, [("PTd", sk), "cb"], [("ps", RB[tt])])

                            steps.append((s1, s2a, s2, s3))
            pipeline(steps, [0, 1, 2, 4])
            for qt in range(2):
                act(rinv[qt][:, :], PS[RB[qt]][:, :], AF.Ln, [("ps", RB[qt])], [("rinv", qt)])
                act(rinv[qt][:, :], rinv[qt][:, :], AF.Exp, [("rinv", qt)], [("rinv", qt)], scale=-1.0)
                tt_("dve", qo(h, qt), PS[OB[qt]][:, :], rinv[qt][:, :], ALU.mult, [("ps", OB[qt]), ("rinv", qt)], [qo_key(h, qt)])
        P.barrier()
        A.reset(m0)

    def wout_phase(l):
        W = (w_out_even if l % 2 == 0 else w_out_odd)[l // 2]

        def consume(c128, tt, b):
            cs = slice(tt * 512, tt * 512 + 512)
            tt_("dve", hT[:, c128, cs], hT[:, c128, cs], PS[b][:, :], ALU.add, [("hT", c128, tt), ("ps", b)], [("hT", c128, tt)])

        proj_fm(W, 8, 0, lambda kc, tt: qo(kc, tt), lambda kc, tt: [qo_key(kc, tt)], KC, consume)

    def ffn_phase(l, j):
        Wu = w_up[l]
        Wd = w_down[l]
        m0 = A.mark()
        ub = {(s, tt): A.alloc("ub", [128, 514], F32) for s in range(2) for tt in range(2)}
        acc = [A.alloc("acc", [128, 512], F32) for _ in range(4)]
        nacc = 0
        for fq in range(4):
            blocks = [(fq * 11 + 2 * i_, 2) for i_ in range(5)] + [(fq * 11 + 10, 1)]
            for (fc0, nsub) in blocks:
                gblk, gkey = load_wblock(Wu[:, fc0 * 128:fc0 * 128 + nsub * 128], KC, nsub * 128)
                vblk, vkey = load_wblock(Wu[:, FF + fc0 * 128:FF + fc0 * 128 + nsub * 128], KC, nsub * 128)
                for sub in range(nsub):
                    fc = fc0 + sub
                    fcl = fc - fq * 11
                    for tt in range(2):
                        res = []
                        for s, (blk, wkey) in enumerate(((gblk, gkey), (vblk, vkey))):
                            b = next_bank()
                            for kc in range(KC):
                                mm(PS[b][:, :], blk[:, kc, sub * 128:(sub + 1) * 128], aT_rhs(kc, tt), kc == 0, kc == KC - 1,
                                   [wkey, ("aT", kc, tt)], [("ps", b)])
                            ch = fc + 44 * s
                            u = ub[(s, tt)]
                            uk = ("ub", s, tt)
                            act(u[:, 2:514], PS[b][:, :], AF.Copy, [("ps", b)], [uk])
                            if tt == 0:
                                cp_("dve", u[:, 0:2], ucar[:, l, ch, :], ["ucar"], [uk])
                            else:
                                cp_("dve", u[:, 0:2], ub[(s, 0)][:, 512:514], [("ub", s, 0)], [uk])
                                cp_("dve", ucar[:, l, ch, :], u[:, 512:514], [uk], ["ucar"])
                            a_ = acc[nacc % 4]
                            ak = ("acc", nacc % 4)
                            nacc += 1
                            ts_("dve", a_[:, :], u[:, 2:514], cw[:, l, 2, ch:ch + 1], cw[:, l, 3, ch:ch + 1], ALU.mult, ALU.add, [uk, "cw"], [ak])
                            stt_("dve", a_[:, :], u[:, 1:513], cw[:, l, 1, ch:ch + 1], a_[:, :], ALU.mult, ALU.add, [uk, ak, "cw"], [ak])
                            stt_("dve", a_[:, :], u[:, 0:512], cw[:, l, 0, ch:ch + 1], a_[:, :], ALU.mult, ALU.add, [uk, ak, "cw"], [ak])
                            res.append((a_, ak))
                        (ag, agk), (av, avk) = res
                        act(ag[:, :], ag[:, :], AF.Silu, [agk], [agk])
                        tt_("dve", X[:, fcl * 2 + tt, :], ag[:, :], av[:, :], ALU.mult, [agk, avk], [("X", fcl * 2 + tt)])

            def consume(c128, tt, b):
                cs = slice(tt * 512, tt * 512 + 512)
                tt_("dve", hT[:, c128, cs], hT[:, c128, cs], PS[b][:, :], ALU.add, [("hT", c128, tt), ("ps", b)], [("hT", c128, tt)])

            proj_fm(Wd[fq * 11 * 128:(fq + 1) * 11 * 128, :], 8, 0, lambda kc, tt: X[:, kc * 2 + tt, :],
                    lambda kc, tt: [("X", kc * 2 + tt)], 11, consume)
        P.barrier()
        A.reset(m0)

    def ple_phase(l, j):
        t0 = j * T
        m0 = A.mark()
        pTc = A.alloc("pTc", [128, 2, T], BF16)
        pTf = A.alloc("pTf", [128, 2, T], F32)
        sg = [A.alloc("sg", [128, 512], F32) for _ in range(3)]
        dma("sp", "pTf", pTf[:, :, :], pT[l, :, t0:t0 + T].rearrange("(k p) t -> p k t", p=128), [], ["pTf"])
        for k_ in range(2):
            act(pTc[:, k_, :], pTf[:, k_, :], AF.Copy, ["pTf"], ["pTc"])
        n = 0
        for cbk in range(8):
            c0 = cbk * 256
            gblk, gkey = load_wblock(w_pg[l][:, c0:c0 + 256], KC, 256)
            pblk, pkey = load_wblock(w_pl[l][:, c0:c0 + 256], 2, 256)
            for sub in range(2):
                dc = cbk * 2 + sub
                for tt in range(2):
                    cs = slice(tt * 512, tt * 512 + 512)
                    bg = next_bank()
                    for kc in range(KC):
                        mm(PS[bg][:, :], gblk[:, kc, sub * 128:(sub + 1) * 128], aT_rhs(kc, tt), kc == 0, kc == KC - 1,
                           [gkey, ("aT", kc, tt)], [("ps", bg)])
                    bp = next_bank()
                    for kc in range(2):
                        mm(PS[bp][:, :], pblk[:, kc, sub * 128:(sub + 1) * 128], pTc[:, kc, cs], kc == 0, kc == 1,
                           [pkey, "pTc"], [("ps", bp)])
                    s_ = sg[n % 3]
                    sk = ("sg", n % 3)
                    n += 1
                    act(s_[:, :], PS[bg][:, :], AF.Sigmoid, [("ps", bg)], [sk])
                    tt_("dve", s_[:, :], s_[:, :], PS[bp][:, :], ALU.mult, [sk, ("ps", bp)], [sk])
                    tt_("dve", hT[:, dc, cs], hT[:, dc, cs], s_[:, :], ALU.add, [("hT", dc, tt), sk], [("hT", dc, tt)])
        P.barrier()
        A.reset(m0)

    for j in range(n_chunks):
        t0 = j * T
        for q in range(4):
            dma("sp", ("hTload", q), hT[:, q * 4:(q + 1) * 4, :],
                xT[q * 512:(q + 1) * 512, t0:t0 + T].rearrange("(k p) t -> p k t", p=128), [],
                [("hT", kc, tt) for kc in range(q * 4, q * 4 + 4) for tt in range(2)])
        for l in range(n_layers):
            if do_mix:
                _ph(f"c{j}l{l}:norm")
                if l % 2 == 1:
                    mk_ = rmsnorm_phase(l, hold=True)
                else:
                    mk_ = rmsnorm_phase(l)
                _ph(f"c{j}l{l}:qkv")
                qkv_phase(l, j)
                A.reset(mk_)
                if l % 2 == 0:
                    _ph(f"c{j}l{l}:sb")
                    sb_attention(l, j, list(range(8)))
                    _ph(f"c{j}l{l}:dil")
                    dil_attention(l, j, list(range(8, 16)))
                else:
                    _ph(f"c{j}l{l}:fox")
                    fox_attention(l, j)
                _ph(f"c{j}l{l}:wout")
                wout_phase(l)
            if do_ffn:
                _ph(f"c{j}l{l}:norm")
                mk_ = rmsnorm_phase(4 + l, hold=True)
                _ph(f"c{j}l{l}:ffn")
                ffn_phase(l, j)
                A.reset(mk_)
            if do_ple:
                _ph(f"c{j}l{l}:norm")
                mk_ = rmsnorm_phase(8 + l, hold=True)
                _ph(f"c{j}l{l}:ple")
                ple_phase(l, j)
                A.reset(mk_)
        _ph(f"c{j}:final")
        rmsnorm_phase(12, final_chunk=j)
    _ph("end")
    P.wait_for("sp", out_pids)

    with ExitStack() as es:
        sems = {e: [es.enter_context(nc.semaphore(f"s_{e}_{i}")) for i in range(NROT)] for e in COMPUTE}
        chan_sems = {c: es.enter_context(nc.semaphore(f"c_{c.name}")) for c in P.chans}
        emit_program(nc, P, sems, chan_sems)
    return nc, A.peak


_CACHE = {}


def _layout_inputs(inp, b):
    f32 = np.float32
    cfc, cbc = make_consts()
    gains = np.zeros((128, 13, KC), f32)
    for n, arr in enumerate(list(inp["attn_norm"]) + list(inp["ffn_norm"]) + list(inp["ple_norm"]) + [inp["final_norm"]]):
        gains[:, n, :] = np.asarray(arr, f32).reshape(KC, 128).T
    cw = np.zeros((128, DEPTH, 4, 88), f32)
    for l in range(DEPTH):
        for t in range(3):
            cw[:, l, t, :] = np.asarray(inp["conv_w"][l, t], f32).reshape(88, 128).T
        cw[:, l, 3, :] = np.asarray(inp["conv_b"][l], f32).reshape(88, 128).T
    m = {
        "xT": np.ascontiguousarray(np.asarray(inp["x"][b], f32).T),
        "pT": np.ascontiguousarray(np.transpose(np.asarray(inp["p"][:, b], f32), (0, 2, 1))),
        "pos": np.ascontiguousarray(np.asarray(inp["positions"][b], np.int32).reshape(1, SEQ)),
        "gains": gains.reshape(128, -1),
        "cw": cw.reshape(128, -1),
        "bfg": np.asarray(inp["b_forget"], f32).reshape(1, 32),
        "cf": cfc,
        "cb": cbc,
    }
    for k in ("w_in_even", "w_out_even", "w_in_odd", "w_out_odd", "w_up", "w_down", "w_ple_gate", "w_ple"):
        m[k] = np.ascontiguousarray(np.asarray(inp[k], f32))
    return m


def run(inputs, n_layers=DEPTH, n_chunks=NCHUNK, do_mix=True, do_ffn=True, do_ple=True, trace=False):
    key = (n_layers, n_chunks, do_mix, do_ffn, do_ple)
    if key not in _CACHE:
        _CACHE[key] = build_program(*key)[0]
    nc = _CACHE[key]
    in_maps = [_layout_inputs(inputs, b) for b in range(NB)]
    res = run_bass_kernel_spmd(nc, in_maps, core_ids=list(range(NB)), trace=trace)
    out = np.stack([np.ascontiguousarray(res.results[b]["yT"].T) for b in range(NB)], axis=0)
    return out.astype(np.float32), res


def kernel(**inputs):
    out, _ = run(inputs)
    return out
```
